# Optimizing a Trainium2 kernel written in Bass

```python
import math
import jax, jax.numpy as jnp
from jax import lax
import numpy as np

D_MODEL = 2048
BATCH = 4
SEQ = 2048
DEPTH = 1

D_MIX = D_MODEL
SSM_D_INNER = D_MIX // 2
SSM_HEAD_DIM = 64
SSM_N_HEADS = SSM_D_INNER // SSM_HEAD_DIM
SSM_N_GROUPS = 4
SSM_D_STATE = 128
SSM_CONV = 4
SSM_CHUNK = 128
SSM_CONV_DIM = SSM_D_INNER + 2 * SSM_N_GROUPS * SSM_D_STATE
ATTN_WIDTH = D_MIX - SSM_D_INNER
ATTN_HEAD_DIM = 64
ATTN_N_HEADS = ATTN_WIDTH // ATTN_HEAD_DIM
ATTN_N_KV = 2
WINDOW = 128
ATTN_BLOCK = WINDOW
D_FF = 4 * D_MODEL
EPS = 1e-5

IN_PROJ_DIM = SSM_D_INNER + SSM_CONV_DIM + SSM_N_HEADS + ATTN_WIDTH + 2 * ATTN_N_KV * ATTN_HEAD_DIM

kernel_name = 'hymba_ssd_swa_sink_hybrid'


def rmsnorm(x, g):
    xf = x.astype(jnp.float32)
    y = xf * lax.rsqrt(jnp.mean(xf * xf, axis=-1, keepdims=True) + EPS)
    return (y * g.astype(jnp.float32)).astype(x.dtype)


def causal_dwconv(u, w, b):
    K, C = w.shape
    out = lax.conv_general_dilated(
        u, w[:, None, :].astype(u.dtype), window_strides=(1,), padding=[(K - 1, 0)],
        dimension_numbers=('NWC', 'WIO', 'NWC'), feature_group_count=C)
    return out + b.astype(u.dtype)


def ssd_chunked(x, dt, A, Bm, Cm):
    b, L, H, P = x.shape
    G, N = Bm.shape[-2:]
    R = H // G
    Q = SSM_CHUNK
    nc = L // Q
    f32 = jnp.float32
    Xd = (x.astype(f32) * dt[..., None]).reshape(b, nc, Q, G, R, P)
    a = jnp.moveaxis((dt * A).reshape(b, nc, Q, G, R), 2, -1)
    a_cs = jnp.cumsum(a, axis=-1)
    Bc = Bm.astype(f32).reshape(b, nc, Q, G, N)
    Cc = Cm.astype(f32).reshape(b, nc, Q, G, N)
    idx = jnp.arange(Q)
    causal = idx[:, None] >= idx[None, :]
    seg = a_cs[..., :, None] - a_cs[..., None, :]
    Lmat = jnp.exp(jnp.where(causal, seg, -jnp.inf))
    CB = jnp.einsum('bclgn,bcsgn->bcgls', Cc, Bc)
    y_diag = jnp.einsum('bcgls,bcgrls,bcsgrp->bclgrp', CB, Lmat, Xd)
    decay_to_end = jnp.exp(a_cs[..., -1:] - a_cs)
    chunk_states = jnp.einsum('bcsgn,bcgrs,bcsgrp->bcgrpn', Bc, decay_to_end, Xd)
    chunk_decay = jnp.exp(a_cs[..., -1])

    def step(h, inp):
        s_c, d_c = inp
        return d_c[..., None, None] * h + s_c, h

    h0 = jnp.zeros((b, G, R, P, N), f32)
    _, prev = lax.scan(step, h0, (jnp.moveaxis(chunk_states, 1, 0), jnp.moveaxis(chunk_decay, 1, 0)))
    prev = jnp.moveaxis(prev, 0, 1)
    y_off = jnp.einsum('bclgn,bcgrpn,bcgrl->bclgrp', Cc, prev, jnp.exp(a_cs))
    return (y_diag + y_off).reshape(b, L, H, P)


def ssd_mixer(z, xBC, dt_raw, conv_w, conv_b, dt_bias, A_log, D_skip, norm_g):
    b, L, _ = z.shape
    f32 = jnp.float32
    xBC = jax.nn.silu(causal_dwconv(xBC, conv_w, conv_b))
    GN = SSM_N_GROUPS * SSM_D_STATE
    xs = xBC[..., :SSM_D_INNER].reshape(b, L, SSM_N_HEADS, SSM_HEAD_DIM)
    Bm = xBC[..., SSM_D_INNER:SSM_D_INNER + GN].reshape(b, L, SSM_N_GROUPS, SSM_D_STATE)
    Cm = xBC[..., SSM_D_INNER + GN:].reshape(b, L, SSM_N_GROUPS, SSM_D_STATE)
    dt = jax.nn.softplus(dt_raw.astype(f32) + dt_bias.astype(f32))
    A = -jnp.exp(A_log.astype(f32))
    y = ssd_chunked(xs, dt, A, Bm, Cm)
    y = y + xs.astype(f32) * D_skip.astype(f32)[:, None]
    y = y.reshape(b, L, SSM_D_INNER) * jax.nn.silu(z.astype(f32))
    yg = y.reshape(b, L, SSM_N_GROUPS, SSM_D_INNER // SSM_N_GROUPS)
    yg = yg * lax.rsqrt(jnp.mean(yg * yg, axis=-1, keepdims=True) + EPS)
    y = yg.reshape(b, L, SSM_D_INNER) * norm_g.astype(f32)
    return y.astype(z.dtype)


def swa_sink_attention(q, k, v, sinks):
    b, L, Hq, Dh = q.shape
    Hkv = k.shape[2]
    R = Hq // Hkv
    Qb = ATTN_BLOCK
    nb = L // Qb
    f32 = jnp.float32
    qb = q.astype(f32).reshape(b, nb, Qb, Hkv, R, Dh)
    pad = jnp.zeros((b, Qb, Hkv, Dh), f32)
    kp = jnp.concatenate([pad, k.astype(f32)], axis=1).reshape(b, nb + 1, Qb, Hkv, Dh)
    vp = jnp.concatenate([pad, v.astype(f32)], axis=1).reshape(b, nb + 1, Qb, Hkv, Dh)
    kb = jnp.concatenate([kp[:, :-1], kp[:, 1:]], axis=2)
    vb = jnp.concatenate([vp[:, :-1], vp[:, 1:]], axis=2)
    s = jnp.einsum('bnqhrd,bnkhd->bnhrqk', qb, kb) * (Dh ** -0.5)
    i = jnp.arange(Qb)[:, None]
    j = jnp.arange(2 * Qb)[None, :]
    n = jnp.arange(nb)[:, None, None]
    diff = Qb + i - j
    valid = (diff >= 0) & (diff < WINDOW) & ((n - 1) * Qb + j >= 0)
    s = jnp.where(valid[None, :, None, None], s, -jnp.inf)
    sink = sinks.astype(f32).reshape(Hkv, R)[None, None, :, :, None, None]
    m = jnp.maximum(jnp.max(s, axis=-1, keepdims=True), sink)
    p = jnp.exp(s - m)
    denom = jnp.sum(p, axis=-1, keepdims=True) + jnp.exp(sink - m)
    o = jnp.einsum('bnhrqk,bnkhd->bnqhrd', p / denom, vb)
    return o.reshape(b, L, Hq * Dh).astype(q.dtype)


def setup_inputs(seed: int = 0) -> dict:
    key = jax.random.key(seed)
    ks = jax.random.split(key, 16)
    f32 = jnp.float32

    def nrm(k, shape, scale):
        return jax.random.normal(k, shape, f32) * scale

    x = nrm(ks[0], (BATCH, SEQ, D_MODEL), 1.0)
    mix_norm_g = 1.0 + nrm(ks[1], (DEPTH, D_MODEL), 0.02)
    w_in = nrm(ks[2], (DEPTH, D_MODEL, IN_PROJ_DIM), D_MODEL ** -0.5)
    conv_w = nrm(ks[3], (DEPTH, SSM_CONV, SSM_CONV_DIM), SSM_CONV ** -0.5)
    conv_b = nrm(ks[4], (DEPTH, SSM_CONV_DIM), 0.01)
    dt0 = jnp.exp(jax.random.uniform(ks[5], (DEPTH, SSM_N_HEADS), f32, math.log(1e-3), math.log(1e-1)))
    dt_bias = dt0 + jnp.log(-jnp.expm1(-dt0))
    A_log = jnp.log(jax.random.uniform(ks[6], (DEPTH, SSM_N_HEADS), f32, 1.0, 16.0))
    D_skip = 1.0 + nrm(ks[7], (DEPTH, SSM_N_HEADS), 0.02)
    ssm_norm_g = 1.0 + nrm(ks[8], (DEPTH, SSM_D_INNER), 0.02)
    attn_sinks = nrm(ks[9], (DEPTH, ATTN_N_HEADS), 0.5)
    attn_out_norm_g = 1.0 + nrm(ks[10], (DEPTH, ATTN_WIDTH), 0.02)
    w_out = nrm(ks[11], (DEPTH, D_MIX, D_MODEL), D_MIX ** -0.5)
    mlp_norm_g = 1.0 + nrm(ks[12], (DEPTH, D_MODEL), 0.02)
    w_up = nrm(ks[13], (DEPTH, D_MODEL, D_FF), D_MODEL ** -0.5)
    w_down = nrm(ks[14], (DEPTH, D_FF, D_MODEL), D_FF ** -0.5)
    final_norm_g = 1.0 + nrm(ks[15], (D_MODEL,), 0.02)
    return {'x': x, 'mix_norm_g': mix_norm_g, 'w_in': w_in, 'conv_w': conv_w, 'conv_b': conv_b,
            'dt_bias': dt_bias, 'A_log': A_log, 'D_skip': D_skip, 'ssm_norm_g': ssm_norm_g,
            'attn_sinks': attn_sinks, 'attn_out_norm_g': attn_out_norm_g, 'w_out': w_out,
            'mlp_norm_g': mlp_norm_g, 'w_up': w_up, 'w_down': w_down, 'final_norm_g': final_norm_g}


def reference(x, mix_norm_g, w_in, conv_w, conv_b, dt_bias, A_log, D_skip, ssm_norm_g,
              attn_sinks, attn_out_norm_g, w_out, mlp_norm_g, w_up, w_down, final_norm_g):
    b, L, _ = x.shape
    KVW = ATTN_N_KV * ATTN_HEAD_DIM
    splits = [SSM_D_INNER, SSM_D_INNER + SSM_CONV_DIM, SSM_D_INNER + SSM_CONV_DIM + SSM_N_HEADS,
              SSM_D_INNER + SSM_CONV_DIM + SSM_N_HEADS + ATTN_WIDTH,
              SSM_D_INNER + SSM_CONV_DIM + SSM_N_HEADS + ATTN_WIDTH + KVW]
    for l in range(DEPTH):
        h = rmsnorm(x, mix_norm_g[l])
        proj = jnp.einsum('bsd,de->bse', h, w_in[l])
        z, xBC, dt_raw, q, k, v = jnp.split(proj, splits, axis=-1)
        y_ssm = ssd_mixer(z, xBC, dt_raw, conv_w[l], conv_b[l], dt_bias[l], A_log[l],
                          D_skip[l], ssm_norm_g[l])
        y_att = swa_sink_attention(q.reshape(b, L, ATTN_N_HEADS, ATTN_HEAD_DIM),
                                   k.reshape(b, L, ATTN_N_KV, ATTN_HEAD_DIM),
                                   v.reshape(b, L, ATTN_N_KV, ATTN_HEAD_DIM), attn_sinks[l])
        y_att = rmsnorm(y_att, attn_out_norm_g[l])
        y = jnp.concatenate([y_ssm, y_att.astype(y_ssm.dtype)], axis=-1)
        x = x + jnp.einsum('bse,ed->bsd', y, w_out[l])
        h = rmsnorm(x, mlp_norm_g[l])
        u = jnp.einsum('bsd,df->bsf', h, w_up[l])
        x = x + jnp.einsum('bsf,fd->bsd', jnp.square(jax.nn.relu(u)), w_down[l])
    return rmsnorm(x, final_norm_g)
```

```python
import numpy as np
import concourse.bass as bass
import concourse.mybir as mybir
from concourse.bass_utils import run_bass_kernel_spmd

F32 = mybir.dt.float32
BF16 = mybir.dt.bfloat16
AF = mybir.ActivationFunctionType
ALU = mybir.AluOpType
AX = mybir.AxisListType

ENGS = ("pe", "dve", "act", "pool", "sp")
EPS = 1e-5
NEG = -30000.0


class Buf:
    __slots__ = ("name", "lw", "rd")

    def __init__(self, name):
        self.name = name
        self.lw = None
        self.rd = []


class Ins:
    __slots__ = ("eng", "fn", "deps", "is_dma", "key", "signal", "rank", "idx")

    def __init__(self, eng, fn, is_dma=False, key=None):
        self.eng = eng
        self.fn = fn
        self.deps = []
        self.is_dma = is_dma
        self.key = key
        self.signal = False
        self.rank = 0
        self.idx = 0


class Sched:
    def __init__(self, nc):
        self.nc = nc
        self.streams = {e: [] for e in ENGS}
        self.keys = {}

    def _deps(self, ins, reads, writes):
        deps = {}
        raw = set()
        for b in reads:
            if b.lw is not None:
                deps[id(b.lw)] = b.lw
                raw.add(id(b.lw))
        for b in writes:
            if b.lw is not None:
                deps[id(b.lw)] = b.lw
            for r in b.rd:
                deps[id(r)] = r
        best = {}
        out = []
        for k, d in deps.items():
            if d is ins:
                continue
            if d.is_dma:
                out.append(d)
                continue
            if (not ins.is_dma) and d.eng == ins.eng:
                if ins.eng == "pe":
                    continue
            cur = best.get(d.eng)
            if cur is None or cur.idx < d.idx:
                best[d.eng] = d
        out.extend(best.values())
        ins.deps = out
        for b in writes:
            b.lw = ins
            b.rd = []
        for b in reads:
            if b.lw is not ins:
                b.rd.append(ins)

    def op(self, eng, fn, reads=(), writes=()):
        ins = Ins(eng, fn)
        ins.idx = len(self.streams[eng])
        self._deps(ins, reads, writes)
        self.streams[eng].append(ins)
        return ins

    def dma(self, eng, key, fn, reads=(), writes=()):
        ins = Ins(eng, fn, is_dma=True, key=key)
        ins.idx = len(self.streams[eng])
        self._deps(ins, reads, writes)
        lst = self.keys.setdefault(key, [])
        lst.append(ins)
        ins.rank = 16 * len(lst)
        ins.signal = True
        self.streams[eng].append(ins)
        return ins

    def barrier_wait(self, eng, ins_list):
        ins = Ins(eng, None)
        ins.deps = list(ins_list)
        ins.idx = len(self.streams[eng])
        self.streams[eng].append(ins)
        return ins


def emit(nc, sched):
    import contextlib

    with contextlib.ExitStack() as st:
        esem = {e: st.enter_context(nc.semaphore("s_" + e)) for e in ENGS}
        ksem = {k: st.enter_context(nc.semaphore("k_%d" % i)) for i, k in enumerate(sched.keys)}
        for e in ENGS:
            for ins in sched.streams[e]:
                for d in ins.deps:
                    d.signal = True
        for e in ENGS:
            r = 0
            for ins in sched.streams[e]:
                if ins.is_dma:
                    continue
                if ins.signal:
                    r += 1
                    ins.rank = r
        block = st.enter_context(nc.Block())

        def run(e, engobj):
            seen = {}
            for ins in sched.streams[e]:
                need = {}
                for d in ins.deps:
                    sem = ksem[d.key] if d.is_dma else esem[d.eng]
                    kk = id(sem)
                    if kk not in need or need[kk][1] < d.rank:
                        need[kk] = (sem, d.rank)
                for kk, (sem, val) in need.items():
                    if seen.get(kk, 0) >= val:
                        continue
                    engobj.wait_ge(sem, val)
                    seen[kk] = val
                if ins.fn is None:
                    continue
                bi = ins.fn(engobj)
                if ins.is_dma:
                    bi.then_inc(ksem[ins.key], 16)
                elif ins.signal:
                    bi.then_inc(esem[e], 1)

        @block.tensor
        def _(eng):
            run("pe", eng)

        @block.vector
        def _(eng):
            run("dve", eng)

        @block.scalar
        def _(eng):
            run("act", eng)

        @block.gpsimd
        def _(eng):
            run("pool", eng)

        @block.sync
        def _(eng):
            run("sp", eng)


P_G1, P_G2, P_GC, P_CW, P_CB, P_DTB, P_ALOG, P_DSK, P_SNK, NPRM = 0, 16, 32, 48, 112, 128, 144, 160, 176, 192
C_ID, C_TRI, C_US, C_MR4, C_AM, C_AM0, NCST = 0, 128, 256, 384, 896, 1152, 1408


class _Stop(Exception):
    pass


def build_nc(n_prefix_blocks=2, n_own_blocks=2, stage=99, dumps=()):
    nc = bass.Bass("TRN2", target_bir_lowering=False)
    S = Sched(nc)

    def chk(st):
        if stage <= st:
            raise _Stop()

    def dram(name, shape, kind="ExternalInput"):
        return nc.dram_tensor(name, shape, F32, kind=kind).ap()

    xo = dram("xo", [1024, 2048])
    xp = dram("xp", [1024, 2048])
    w_in = dram("w_in", [8, 128, 8192])
    w_kv = dram("w_kv", [128, 4096])
    w_out = dram("w_out", [4, 128, 8192])
    w_up = dram("w_up", [16, 128, 8192])
    w_down = dram("w_down", [16, 128, 8192])
    prm_d = dram("prm", [128, NPRM])
    cst_d = dram("cst", [128, NCST])
    fgr_d = dram("fgr", [128, 2048])
    flg_d = dram("flg", [128, 1])
    wdt_d = dram("wdt", [128, 256])
    out_d = dram("out", [1024, 2048], kind="ExternalOutput")

    def wblk(t, i):
        return t[i].rearrange("p (k c) -> p k c", k=16)

    WIN_BLK = {1024: 0, 1536: 1, 2048: 2, 2560: 3, 3088: 4, 3600: 5, 0: 6, 512: 7}

    def sb(name, shape, dt=F32):
        return nc.alloc_sbuf_tensor(name, shape, dt)

    prm = sb("prm_s", [128, NPRM]); b_prm = Buf("prm")
    cst = sb("cst_s", [128, NCST]); b_cst = Buf("cst")
    fgr = sb("fgr_s", [128, 2048]); b_fgr = Buf("fgr")
    flg = sb("flg_s", [128, 1]); b_flg = Buf("flg")
    identb = sb("identb", [128, 128], BF16)
    mr4b = sb("mr4b", [128, 512], BF16)
    amb = sb("amb", [128, 256], BF16)
    am0b = sb("am0b", [128, 256], BF16)
    ones = sb("ones", [128, 128])
    cvals = sb("cvals", [128, 4])
    aneg = sb("aneg", [128, 16])
    b_c2 = Buf("consts2")
    wdtf = sb("wdtf", [128, 256]); b_wdtf = Buf("wdtf")
    wdtb = sb("wdtb", [128, 16, 16], BF16)
    Sst = sb("Sst", [128, 1024]); b_S = Buf("S")
    Sbf = sb("Sbf", [128, 1024], BF16); b_Sbf = Buf("Sbf")
    tailu = sb("tailu", [128, 16, 3]); b_tail = [Buf("tail%d" % c) for c in range(16)]
    KP = [sb("KP%d" % i, [128, 5 * 128], BF16) for i in range(4)]
    b_KP = [Buf("KPs%d" % s) for s in range(5)]
    vtok = sb("vtok", [128, 5, 128], BF16)
    b_v = [Buf("v%d" % s) for s in range(5)]

    NW = 4
    Wt = [sb("W%d" % i, [128, 8, 512], BF16) for i in range(NW)]
    b_W = [Buf("W%d" % i) for i in range(NW)]
    wctr = [0]

    class WBlock:
        def __init__(self, idx):
            self.idx = idx

        def __getitem__(self, key):
            _, kc, cs = key
            return Wt[self.idx[kc // 8]][:, kc % 8, cs]

    def load_w(parts):
        (lo, hi, src), = parts
        idx = []
        for h in range(2):
            i = wctr[0] % NW
            wctr[0] += 1
            idx.append(i)
            S.dma("pool", "w%d" % i, lambda e, o=Wt[i][:, :, lo:hi], s=src[:, h * 8:(h + 1) * 8, :]: e.dma_start(out=o, in_=s),
                  writes=[b_W[i]])
        return WBlock(idx), [b_W[idx[0]], b_W[idx[1]]]

    hy = sb("hy", [128, 16, 512], BF16)
    b_hy = [Buf("hy%d" % t) for t in range(4)]
    X1 = sb("X1", [128, 8192])
    x1v = X1[:, :].rearrange("p (t c) -> p t c", t=4)
    b_x1 = [Buf("x1_%d" % t) for t in range(4)]
    xBCT = X1[:, 0:4096].bitcast(BF16).rearrange("p (c n) -> p c n", c=16)
    b_xbc = [Buf("xbc%d" % c) for c in range(16)]
    xtok = X1[:, 4096:6144].bitcast(BF16).rearrange("p (t c) -> p t c", t=4)
    b_xtok = [Buf("xtok%d" % t) for t in range(4)]
    Btok = X1[:, 6144:7168].bitcast(BF16).rearrange("p (t c) -> p t c", t=4)
    b_btok = [Buf("btok%d" % t) for t in range(4)]
    cvb = [X1[:, 7168:7680], X1[:, 7680:8192]]
    b_cv = [Buf("cv0"), Buf("cv1")]
    fence_d = sb("fence_d", [128, 4]); b_fence = Buf("fence")

    def fence(old, new):
        S.op("dve", lambda e: e.memset(fence_d[:, 0:1], 0.0), writes=list(old) + list(new) + [b_fence])

    AR = sb("AR", [128, 22528])
    off = [0]

    def carve(nwords):
        a = off[0]
        off[0] += nwords
        assert off[0] <= 22528, off[0]
        return AR[:, a:a + nwords]

    sz = carve(2048).bitcast(BF16).rearrange("p (t c) -> p t c", t=4)
    b_sz = [Buf("sz%d" % t) for t in range(4)]
    qT = carve(2048).bitcast(BF16).rearrange("p (c n) -> p c n", c=8)
    b_qT = [Buf("qT%d" % c) for c in range(8)]
    ytile = [carve(1024).bitcast(BF16) for _ in range(2)]
    b_yt = [[Buf("yts%d" % i), Buf("yta%d" % i)] for i in range(2)]
    dtt = carve(64).rearrange("p (t h) -> p t h", t=4); b_dt = [Buf("dt%d" % t) for t in range(4)]
    att = carve(64).rearrange("p (t h) -> p t h", t=4); b_a = [Buf("a%d" % t) for t in range(4)]
    xt0_ = carve(2048)
    xn = carve(1024).bitcast(BF16); b_xn = Buf("xn")
    junk = carve(512).bitcast(BF16); b_junk = Buf("junk")
    ss = carve(8); b_ss = Buf("ss")
    ub = [carve(516), carve(516)]; b_u = [Buf("u0"), Buf("u1")]
    ex = carve(48); b_ex = Buf("ex")
    sp_tmp = carve(16); b_sp = Buf("sp_tmp")
    aTri2 = carve(2048); aTri = aTri2.rearrange("p (h l) -> p h l", h=16); b_aTri = Buf("aTri")
    LT2 = carve(1024).bitcast(BF16); LT = LT2.rearrange("p (h l) -> p h l", h=16); b_LT = Buf("LT")
    MT = carve(1024).bitcast(BF16).rearrange("p (h l) -> p h l", h=16); b_MT = Buf("MT")
    Xd = carve(512).bitcast(BF16); b_Xd = Buf("Xd")
    Xdd = carve(512).bitcast(BF16); b_Xdd = Buf("Xdd")
    Yt = carve(1024); b_Y = Buf("Y")
    gss = carve(16); b_gss = Buf("gss")
    pS_reg = carve(2048); b_p = Buf("p")
    pS = pS_reg.bitcast(BF16).rearrange("p (h k) -> p h k", h=16)
    xt = [xt0_, pS_reg]; b_xt = [Buf("xt0"), b_p]
    pTt = carve(2048).bitcast(BF16).rearrange("p (h k) -> p h k", h=16); b_pT = Buf("pT")
    rmax = carve(16); negm = carve(16); rsum = carve(16); mrow = carve(16); esk = carve(16); rden = carve(16)
    b_sm = Buf("softmax_small")
    OA = carve(1024); b_OA = Buf("OA")
    XDk = aTri2[:, 1024:2048]; b_XDk = b_aTri
    mixer_top = off[0]
    mixer_bufs = (b_sz + b_qT + [b for pr in b_yt for b in pr] + b_dt + b_a + [b_xt[0], b_xn, b_junk, b_ss] + b_u +
                  [b_ex, b_sp, b_aTri, b_LT, b_MT, b_Xd, b_Xdd, b_Y, b_gss, b_p, b_pT, b_sm, b_OA])
    off[0] = 0
    h2_reg = carve(4096)
    h2T = h2_reg.bitcast(BF16).rearrange("p (c n) -> p c n", c=16)
    b_h2 = [Buf("h2_%d" % t) for t in range(4)]
    aT = carve(16384).bitcast(BF16).rearrange("p (c n) -> p c n", c=64)
    b_aT = [Buf("aT%d" % c) for c in range(64)]
    ot = h2_reg[:, 0:2048]
    xth = h2_reg[:, 2048:4096]; b_xth = Buf("xth")
    xn2 = carve(1024).bitcast(BF16); b_xn2 = Buf("xn2")
    ss2 = carve(8); b_ss2 = Buf("ss2")
    rl = [carve(256).bitcast(BF16), carve(256).bitcast(BF16)]; b_rl = [Buf("rl0"), Buf("rl1")]
    mlp_bufs = b_h2 + b_aT + [b_xn2, b_ss2, b_xth] + b_rl

    banks = [nc.alloc_psum_tensor("bank%d" % i, [128, 512], F32) for i in range(8)]
    b_bank = [Buf("bank%d" % i) for i in range(8)]
    bctr = [0]

    reserved = set()

    def nb(hold=False):
        for _ in range(17):
            i = bctr[0] % 8
            bctr[0] += 1
            if i not in reserved:
                break
        else:
            raise RuntimeError("all PSUM banks reserved")
        if hold:
            reserved.add(i)
        return banks[i], b_bank[i]

    def rel(*pairs):
        for (_, bb) in pairs:
            reserved.discard(b_bank.index(bb))

    S.dma("sp", "c0", lambda e: e.dma_start(out=prm[:, :], in_=prm_d[:, :]), writes=[b_prm])
    S.dma("sp", "c1", lambda e: e.dma_start(out=cst[:, :], in_=cst_d[:, :]), writes=[b_cst])
    S.dma("sp", "c2", lambda e: e.dma_start(out=fgr[:, :], in_=fgr_d[:, :]), writes=[b_fgr])
    S.dma("sp", "c3", lambda e: e.dma_start(out=flg[:, :], in_=flg_d[:, :]), writes=[b_flg])
    S.dma("sp", "c4", lambda e: e.dma_start(out=wdtf[:, :], in_=wdt_d[:, :]), writes=[b_wdtf])
    S.op("dve", lambda e: e.tensor_copy(wdtb[:, :, :], wdtf[:, :].rearrange("p (k c) -> p k c", k=16)), reads=[b_wdtf], writes=[b_c2])
    S.op("dve", lambda e: e.tensor_copy(identb[:, :], cst[:, C_ID:C_ID + 128]), reads=[b_cst], writes=[b_c2])
    S.op("dve", lambda e: e.tensor_copy(mr4b[:, :], cst[:, C_MR4:C_MR4 + 512]), reads=[b_cst], writes=[b_c2])
    S.op("dve", lambda e: e.tensor_copy(amb[:, :], cst[:, C_AM:C_AM + 256]), reads=[b_cst], writes=[b_c2])
    S.op("dve", lambda e: e.tensor_copy(am0b[:, :], cst[:, C_AM0:C_AM0 + 256]), reads=[b_cst], writes=[b_c2])
    S.op("dve", lambda e: e.memset(ones[:, :], 1.0), writes=[b_c2])
    S.op("dve", lambda e: e.memset(cvals[:, 0:1], EPS), writes=[b_c2])
    S.op("dve", lambda e: e.memset(cvals[:, 1:2], 1.0), writes=[b_c2])
    S.op("dve", lambda e: e.memset(cvals[:, 2:3], 0.0), writes=[b_c2])
    S.op("act", lambda e: e.activation(aneg[:, :], prm[:, P_ALOG:P_ALOG + 16], AF.Exp), reads=[b_prm], writes=[b_c2])
    S.op("dve", lambda e: e.tensor_scalar(aneg[:, :], aneg[:, :], -1.0, None, ALU.mult), reads=[b_c2], writes=[b_c2])
    S.op("dve", lambda e: e.memset(Sst[:, :], 0.0), writes=[b_S])
    S.op("dve", lambda e: e.memset(Sbf[:, :], 0.0), writes=[b_Sbf])
    S.op("dve", lambda e: e.memset(tailu[:, :, :], 0.0), writes=b_tail)
    for i in range(4):
        S.op("dve", lambda e, i=i: e.memset(KP[i][:, :], 0.0), writes=b_KP)
    S.op("dve", lambda e: e.memset(vtok[:, :, :], 0.0), writes=b_v)
    identf = cst[:, C_ID:C_ID + 128]
    tri = cst[:, C_TRI:C_TRI + 128]
    ustr = cst[:, C_US:C_US + 128]
    eps_ap = cvals[:, 0:1]
    one_ap = cvals[:, 1:2]

    def rms_T(*a):
        for _ in rms_T_gen(*a):
            pass

    def rms_T_gen(xin, xin_buf, gcol, dst, dst_bufs, t, xn_, b_xn_, ss_, b_ss_):
        S.op("act", lambda e: e.activation(xn_[:, :], xin, AF.Square, accum_out=ss_[:, 0:1]),
             reads=[xin_buf], writes=[b_xn_, b_ss_])
        S.op("act", lambda e: e.activation(ss_[:, 1:2], ss_[:, 0:1], AF.Ln, bias=eps_ap, scale=1.0 / 2048),
             reads=[b_ss_, b_c2], writes=[b_ss_])
        S.op("act", lambda e: e.activation(ss_[:, 2:3], ss_[:, 1:2], AF.Exp, scale=-0.5), reads=[b_ss_], writes=[b_ss_])
        S.op("act", lambda e: e.activation(xn_[:, :], xin, AF.Copy, scale=ss_[:, 2:3]),
             reads=[xin_buf, b_ss_], writes=[b_xn_])
        yield
        for half in range(2):
            bk, bb = nb(hold=True)
            bkb = bk[:, :].bitcast(BF16)
            for kc in range(8):
                c = half * 8 + kc
                S.op("pe", lambda e, o=bkb[:, kc * 128:(kc + 1) * 128], i=xn_[:, c * 128:(c + 1) * 128]:
                     e.transpose(o, i, identb[:, :]), reads=[b_xn_, b_c2], writes=[bb])
                if kc % 4 == 3:
                    yield
            S.op("dve", lambda e, o=dst[:, half * 8:(half + 1) * 8, t * 128:(t + 1) * 128],
                 i=bkb.rearrange("p (a b) -> p a b", a=8),
                 g=prm[:, gcol + half * 8:gcol + (half + 1) * 8].unsqueeze(2).to_broadcast([128, 8, 128]):
                 e.tensor_tensor(o, i, g, ALU.mult), reads=[bb, b_prm], writes=[dst_bufs[t]])
            rel((bk, bb))
            yield

    def run_il(gens, producers=None):
        gens = list(gens)
        producers = producers or {}
        blocked = {}
        while gens:
            for g_ in list(gens):
                tok = blocked.get(id(g_))
                if tok is not None:
                    if producers.get(tok) in gens:
                        continue
                    del blocked[id(g_)]
                try:
                    r_ = next(g_)
                except StopIteration:
                    gens.remove(g_)
                    continue
                if isinstance(r_, str) and producers.get(r_) in gens:
                    blocked[id(g_)] = r_

    def m1_gen(xsrc, xts, b_xts, xn_, b_xn_, ss_, b_ss_, keyp):
        for t in range(4):
            i = t % len(xts)
            S.dma("sp", keyp + str(i), lambda e, o=xts[i], s_=xsrc[t * 128:(t + 1) * 128, :]: e.dma_start(out=o, in_=s_),
                  writes=[b_xts[i]])
            yield from rms_T_gen(xts[i], b_xts[i], P_G1, hy, b_hy, t, xn_, b_xn_, ss_, b_ss_)

    dbg = {}

    def std_m1(xsrc):
        return m1_gen(xsrc, [xt[0][:, :], xt[1][:, :]], b_xt, xn, b_xn, ss, b_ss, "xt")

    def mixer_block(xsrc, kind, first_own, skip_m1=False, next_m1=None):
        own = kind == "own"
        last_prefix = kind == "prefix_last"
        if not skip_m1:
            run_il([std_m1(xsrc)])

        chk(1)
        def fm_block(col0, nchunk, evac):
            assert nchunk == 4
            W, bW = load_w([(0, 512, wblk(w_in, WIN_BLK[col0]))])
            for cl in range(nchunk):
                bk, bb = nb()
                for kc in range(16):
                    S.op("pe", lambda e, o=bk[:, :], l=W[:, kc, cl * 128:(cl + 1) * 128], r=hy[:, kc, :], kc=kc:
                         e.matmul(o, lhsT=l, rhs=r, start=(kc == 0), stop=(kc == 15)), reads=[bW[kc // 8]] + b_hy, writes=[bb])
                evac(cl, bk, bb)

        def conv_evac(cbase):
            def f(cl, bk, bb):
                c = cbase + cl
                i = c % 2
                u, bu = ub[i], b_u[i]
                cv, bcv = cvb[i], b_cv[i]
                S.op("dve", lambda e: e.tensor_copy(u[:, 0:3], tailu[:, c, :]), reads=[b_tail[c]], writes=[bu])
                S.op("act", lambda e: e.copy(u[:, 3:515], bk[:, :]), reads=[bb], writes=[bu])
                S.op("dve", lambda e: e.tensor_copy(tailu[:, c, :], u[:, 512:515]), reads=[bu], writes=[b_tail[c]])
                S.op("dve", lambda e: e.tensor_scalar(cv, u[:, 0:512], prm[:, P_CW + c * 4:P_CW + c * 4 + 1],
                                                      prm[:, P_CB + c:P_CB + c + 1], ALU.mult, ALU.add),
                     reads=[bu, b_prm], writes=[bcv])
                for k in range(1, 4):
                    S.op("dve", lambda e, k=k: e.scalar_tensor_tensor(cv, u[:, k:k + 512], prm[:, P_CW + c * 4 + k:P_CW + c * 4 + k + 1],
                                                                      cv, ALU.mult, ALU.add),
                         reads=[bu, b_prm, bcv], writes=[bcv])
                S.op("act", lambda e: e.activation(xBCT[:, c, :], cv, AF.Silu), reads=[bcv], writes=[b_xbc[c]])
            return f

        fm_block(1024, 4, conv_evac(0))
        fm_block(1536, 4, conv_evac(4))
        fm_block(2048, 4, conv_evac(8))
        if own or last_prefix:
            fm_block(2560, 4, conv_evac(12))

        chk(2)
        W, bW = load_w([(0, 256, w_kv.rearrange("p (k c) -> p k c", k=16))])
        if own or last_prefix:
            bk, bb = nb()
            for kc in range(16):
                S.op("pe", lambda e, o=bk[:, :], l=W[:, kc, 0:128], r=hy[:, kc, :], kc=kc:
                     e.matmul(o, lhsT=l, rhs=r, start=(kc == 0), stop=(kc == 15)), reads=[bW[kc // 8]] + b_hy, writes=[bb])
            S.op("act", lambda e, o=KP[0][0:64, 128:640], i=bk[0:64, :]: e.copy(o, i), reads=[bb], writes=b_KP[1:5])
            S.op("act", lambda e, o=KP[3][64:128, 128:640], i=bk[64:128, :]: e.copy(o, i), reads=[bb], writes=b_KP[1:5])
            S.dma("sp", "kp1", lambda e: e.dma_start(out=KP[1][64:128, 128:640], in_=KP[0][0:64, 128:640]), reads=b_KP[1:5], writes=b_KP[1:5])
            S.dma("sp", "kp2", lambda e: e.dma_start(out=KP[2][0:64, 128:640], in_=KP[3][64:128, 128:640]), reads=b_KP[1:5], writes=b_KP[1:5])
        for t in range(4):
            bk, bb = nb()
            if own or last_prefix:
                for kc in range(16):
                    S.op("pe", lambda e, o=bk[:, 0:128], l=hy[:, kc, t * 128:(t + 1) * 128], r=W[:, kc, 128:256], kc=kc:
                         e.matmul(o, lhsT=l, rhs=r, start=(kc == 0), stop=(kc == 15)), reads=[bW[kc // 8], b_hy[t]], writes=[bb])
            for kc in range(16):
                S.op("pe", lambda e, o=bk[:, 128:144], l=hy[:, kc, t * 128:(t + 1) * 128], r=wdtb[:, kc, :], kc=kc:
                     e.matmul(o, lhsT=l, rhs=r, start=(kc == 0), stop=(kc == 15)), reads=[b_c2, b_hy[t]], writes=[bb])
            if own or last_prefix:
                S.op("act", lambda e, o=vtok[:, 1 + t, :], i=bk[:, 0:128]: e.copy(o, i), reads=[bb], writes=[b_v[1 + t]])
            S.op("dve", lambda e, i=bk[:, 128:144]: e.tensor_tensor(sp_tmp[:, :], i, prm[:, P_DTB:P_DTB + 16], ALU.add),
                 reads=[bb, b_prm], writes=[b_sp])
            S.op("act", lambda e: e.activation(sp_tmp[:, :], sp_tmp[:, :], AF.Exp), reads=[b_sp], writes=[b_sp])
            S.op("act", lambda e, o=dtt[:, t, :]: e.activation(o, sp_tmp[:, :], AF.Ln, bias=one_ap), reads=[b_sp, b_c2], writes=[b_dt[t]])
            S.op("dve", lambda e, o=att[:, t, :], i=dtt[:, t, :]: e.tensor_tensor(o, i, aneg[:, :], ALU.mult),
                 reads=[b_dt[t], b_c2], writes=[b_a[t]])

        chk(3)

        def zproj_gen():
            for zb in range(2):
                W, bW = load_w([(0, 512, wblk(w_in, WIN_BLK[zb * 512]))])
                for t in range(4):
                    bk, bb = nb(hold=True)
                    for kc in range(16):
                        S.op("pe", lambda e, o=bk[:, :], l=hy[:, kc, t * 128:(t + 1) * 128], r=W[:, kc, 0:512], kc=kc:
                             e.matmul(o, lhsT=l, rhs=r, start=(kc == 0), stop=(kc == 15)), reads=[bW[kc // 8], b_hy[t]], writes=[bb])
                        if kc % 4 == 3:
                            yield
                    S.op("act", lambda e, o=sz[:, t, zb * 512:(zb + 1) * 512], i=bk[:, :]: e.activation(o, i, AF.Silu),
                         reads=[bb], writes=[b_sz[t]])
                    rel((bk, bb))
                    yield

        chk(4)
        def ssd_gen(t):
            tc = slice(t * 128, (t + 1) * 128)
            bk0, bb0 = nb()
            bk0b = bk0[:, :].bitcast(BF16)
            for c in range(8):
                S.op("pe", lambda e, o=bk0b[:, c * 128:(c + 1) * 128], i=xBCT[:, c, tc]: e.transpose(o, i, identb[:, :]),
                     reads=[b_xbc[c], b_c2], writes=[bb0])
            S.op("dve", lambda e, o=xtok[:, t, :], i=bk0b: e.tensor_copy(o, i), reads=[bb0], writes=[b_xtok[t]])
            bk1, bb1 = nb()
            bk1b = bk1[:, :].bitcast(BF16)
            for c in range(4):
                S.op("pe", lambda e, o=bk1b[:, c * 128:(c + 1) * 128], i=xBCT[:, 8 + c, tc]: e.transpose(o, i, identb[:, :]),
                     reads=[b_xbc[8 + c], b_c2], writes=[bb1])
            S.op("dve", lambda e, o=Btok[:, t, :], i=bk1b[:, 0:512]: e.tensor_copy(o, i), reads=[bb1], writes=[b_btok[t]])

            yield
            chk(5)
            bka, bba = nb()
            a_t = att[:, t, :]
            S.op("pe", lambda e, o=bka[:, 0:16]: e.matmul(o, lhsT=tri, rhs=a_t, start=True, stop=True), reads=[b_cst, b_a[t]], writes=[bba])
            S.op("pe", lambda e, o=bka[:, 16:32]: e.matmul(o, lhsT=ones[:, :], rhs=a_t, start=True, stop=True), reads=[b_c2, b_a[t]], writes=[bba])
            S.op("pe", lambda e, o=bka[:, 32:48]: e.matmul(o, lhsT=ustr, rhs=a_t, start=True, stop=True), reads=[b_cst, b_a[t]], writes=[bba])
            S.op("act", lambda e, i=bka[:, 0:48]: e.activation(ex[:, :], i, AF.Exp), reads=[bba], writes=[b_ex])
            eacs_b = ex[:, 0:16].unsqueeze(2).to_broadcast([128, 16, 64])
            cdec_b = ex[:, 16:32].unsqueeze(2).to_broadcast([128, 16, 64])
            dte_b = ex[:, 32:48].unsqueeze(2).to_broadcast([128, 16, 64])
            dt_b = dtt[:, t, :].unsqueeze(2).to_broadcast([128, 16, 64])
            xt3 = xtok[:, t, :].rearrange("p (h d) -> p h d", h=16)
            Xd3 = Xd.rearrange("p (h d) -> p h d", h=16)
            Xdd3 = Xdd.rearrange("p (h d) -> p h d", h=16)
            S.op("dve", lambda e: e.tensor_tensor(Xd3, xt3, dt_b, ALU.mult), reads=[b_xtok[t], b_dt[t]], writes=[b_Xd])
            S.op("dve", lambda e: e.tensor_tensor(Xdd3, Xd3, dte_b, ALU.mult), reads=[b_Xd, b_ex], writes=[b_Xdd])

            yield
            if own:
                S.op("dve", lambda e: e.tensor_tensor(aTri, tri.unsqueeze(1).to_broadcast([128, 16, 128]),
                                                      a_t.unsqueeze(2).to_broadcast([128, 16, 128]), ALU.mult),
                     reads=[b_cst, b_a[t]], writes=[b_aTri])
                for g in range(4):
                    bk, bb = nb()
                    S.op("pe", lambda e, o=bk[:, :], r=aTri2[:, g * 512:(g + 1) * 512]: e.matmul(o, lhsT=ustr, rhs=r, start=True, stop=False),
                         reads=[b_cst, b_aTri], writes=[bb])
                    S.op("pe", lambda e, o=bk[:, :]: e.matmul(o, lhsT=identb[:, :], rhs=mr4b[:, :], start=False, stop=True),
                         reads=[b_c2], writes=[bb])
                    S.op("act", lambda e, o=LT2[:, g * 512:(g + 1) * 512], i=bk[:, :]:
                         e.activation(o, i, AF.Exp), reads=[bb], writes=[b_LT])
                    yield
                bkc, bbc = nb()
                for g in range(4):
                    S.op("pe", lambda e, o=bkc[:, g * 128:(g + 1) * 128], l=xBCT[:, 8 + g, tc], r=xBCT[:, 12 + g, tc]:
                         e.matmul(o, lhsT=l, rhs=r, start=True, stop=True), reads=[b_xbc[8 + g], b_xbc[12 + g]], writes=[bbc])
                for g in range(4):
                    S.op("dve", lambda e, o=MT[:, 4 * g:4 * g + 4, :], i=LT[:, 4 * g:4 * g + 4, :],
                         c=bkc[:, g * 128:(g + 1) * 128].unsqueeze(1).to_broadcast([128, 4, 128]):
                         e.tensor_tensor(o, i, c, ALU.mult), reads=[b_LT, bbc], writes=[b_MT])
                yield
                yd = [nb(hold=True), nb(hold=True)]
                for h in range(16):
                    bk, bb = yd[h // 8]
                    S.op("pe", lambda e, o=bk[:, (h % 8) * 64:(h % 8) * 64 + 64], l=MT[:, h, :], r=Xd[:, h * 64:(h + 1) * 64]:
                         e.matmul(o, lhsT=l, rhs=r, start=True, stop=True), reads=[b_MT, b_Xd], writes=[bb])
                yield
                yo = [nb(hold=True), nb(hold=True)]
                for g in range(4):
                    bk, bb = yo[g // 2]
                    S.op("pe", lambda e, o=bk[:, (g % 2) * 256:(g % 2) * 256 + 256], l=xBCT[:, 12 + g, tc], r=Sbf[:, g * 256:(g + 1) * 256]:
                         e.matmul(o, lhsT=l, rhs=r, start=True, stop=True), reads=[b_xbc[12 + g], b_Sbf], writes=[bb])
                yield
                for hf in range(2):
                    S.op("dve", lambda e, o=Yt[:, hf * 512:(hf + 1) * 512].rearrange("p (h d) -> p h d", h=8),
                         i=yo[hf][0][:, :].rearrange("p (h d) -> p h d", h=8),
                         s=ex[:, hf * 8:hf * 8 + 8].unsqueeze(2).to_broadcast([128, 8, 64]):
                         e.tensor_tensor(o, i, s, ALU.mult), reads=[yo[hf][1], b_ex], writes=[b_Y])
                    S.op("dve", lambda e, o=Yt[:, hf * 512:(hf + 1) * 512], i=yd[hf][0][:, :]:
                         e.tensor_tensor(o, o, i, ALU.add), reads=[yd[hf][1], b_Y], writes=[b_Y])
                rel(*yd)
                rel(*yo)
                yield
                S.op("dve", lambda e: e.tensor_tensor(XDk.rearrange("p (h d) -> p h d", h=16), xt3,
                                                      prm[:, P_DSK:P_DSK + 16].unsqueeze(2).to_broadcast([128, 16, 64]), ALU.mult),
                     reads=[b_xtok[t], b_prm], writes=[b_XDk])
                S.op("dve", lambda e: e.tensor_tensor(Yt, Yt, XDk, ALU.add), reads=[b_Y, b_XDk], writes=[b_Y])
                yield "need_z"
                S.op("dve", lambda e: e.tensor_tensor(Yt, Yt, sz[:, t, :], ALU.mult), reads=[b_Y, b_sz[t]], writes=[b_Y])
                yield
                for g in range(4):
                    S.op("act", lambda e, i=Yt[:, g * 256:(g + 1) * 256], o=junk[:, 0:256], a=gss[:, g:g + 1]:
                         e.activation(o, i, AF.Square, accum_out=a), reads=[b_Y], writes=[b_junk, b_gss])
                S.op("act", lambda e: e.activation(gss[:, 4:8], gss[:, 0:4], AF.Ln, bias=eps_ap, scale=1.0 / 256),
                     reads=[b_gss, b_c2], writes=[b_gss])
                S.op("act", lambda e: e.activation(gss[:, 8:12], gss[:, 4:8], AF.Exp, scale=-0.5), reads=[b_gss], writes=[b_gss])
                yti = ytile[t % 2]
                S.op("dve", lambda e, o=yti[:, 0:1024].rearrange("p (g c) -> p g c", g=4):
                     e.tensor_tensor(o, Yt.rearrange("p (g c) -> p g c", g=4), gss[:, 8:12].unsqueeze(2).to_broadcast([128, 4, 256]), ALU.mult),
                     reads=[b_Y, b_gss], writes=[b_yt[t % 2][0]])

            yield
            chk(6)
            st = [nb(), nb()]
            for g in range(4):
                bk, bb = st[g // 2]
                S.op("pe", lambda e, o=bk[:, (g % 2) * 256:(g % 2) * 256 + 256], l=Btok[:, t, g * 128:(g + 1) * 128], r=Xdd[:, g * 256:(g + 1) * 256]:
                     e.matmul(o, lhsT=l, rhs=r, start=True, stop=True), reads=[b_btok[t], b_Xdd], writes=[bb])
            S.op("dve", lambda e: e.tensor_tensor(Sst[:, :].rearrange("p (h d) -> p h d", h=16),
                                                  Sst[:, :].rearrange("p (h d) -> p h d", h=16), cdec_b, ALU.mult),
                 reads=[b_S, b_ex], writes=[b_S])
            for hf in range(2):
                S.op("dve", lambda e, o=Sst[:, hf * 512:(hf + 1) * 512], i=st[hf][0][:, :]: e.tensor_tensor(o, o, i, ALU.add),
                     reads=[b_S, st[hf][1]], writes=[b_S])
            S.op("dve", lambda e: e.tensor_copy(Sbf[:, :], Sst[:, :]), reads=[b_S], writes=[b_Sbf])

            yield

        def attn_gen(t):
            tc = slice(t * 128, (t + 1) * 128)
            yti = ytile[t % 2]
            chk(7)
            if own:
                mk = am0b if (first_own and t == 0) else amb
                for j in range(8):
                    bk, bb = nb()
                    for half in range(2):
                        h = 2 * j + half
                        pad = KP[(h // 8) * 2 + half]
                        S.op("pe", lambda e, o=bk[:, half * 256:(half + 1) * 256], l=qT[:, j, tc], r=pad[:, t * 128:(t + 2) * 128]:
                             e.matmul(o, lhsT=l, rhs=r, start=True, stop=False), reads=[b_qT[j], b_KP[t], b_KP[t + 1]], writes=[bb])
                        S.op("pe", lambda e, o=bk[:, half * 256:(half + 1) * 256]: e.matmul(o, lhsT=identb[:, :], rhs=mk[:, :], start=False, stop=True),
                             reads=[b_c2], writes=[bb])
                    S.op("dve", lambda e, o=rmax[:, 2 * j:2 * j + 2], i=bk[:, :].rearrange("p (a k) -> p a k", a=2):
                         e.tensor_reduce(o, i, AX.X, ALU.max), reads=[bb], writes=[b_sm])
                    S.op("dve", lambda e, o=mrow[:, 2 * j:2 * j + 2], i=rmax[:, 2 * j:2 * j + 2], s=prm[:, P_SNK + 2 * j:P_SNK + 2 * j + 2]:
                         e.scalar_tensor_tensor(o, i, 0.125, s, ALU.mult, ALU.max), reads=[b_sm, b_prm], writes=[b_sm])
                    S.op("dve", lambda e, o=negm[:, 2 * j:2 * j + 2], i=mrow[:, 2 * j:2 * j + 2]: e.tensor_scalar(o, i, -1.0, None, ALU.mult),
                         reads=[b_sm], writes=[b_sm])
                    for half in range(2):
                        h = 2 * j + half
                        S.op("act", lambda e, o=pS[:, h, :], i=bk[:, half * 256:(half + 1) * 256], b=negm[:, h:h + 1], a=rsum[:, h:h + 1]:
                             e.activation(o, i, AF.Exp, bias=b, scale=0.125, accum_out=a), reads=[bb, b_sm], writes=[b_p, b_sm])
                    yield
                S.op("dve", lambda e: e.tensor_tensor(esk[:, :], prm[:, P_SNK:P_SNK + 16], mrow[:, :], ALU.subtract), reads=[b_sm, b_prm], writes=[b_sm])
                S.op("act", lambda e: e.activation(esk[:, :], esk[:, :], AF.Exp), reads=[b_sm], writes=[b_sm])
                S.op("dve", lambda e: e.tensor_tensor(rden[:, :], rsum[:, :], esk[:, :], ALU.add), reads=[b_sm], writes=[b_sm])
                S.op("dve", lambda e: e.reciprocal(rden[:, :], rden[:, :]), reads=[b_sm], writes=[b_sm])
                for q4 in range(4):
                    bk, bb = nb()
                    bkb = bk[:, :].bitcast(BF16)
                    for hh in range(4):
                        h = q4 * 4 + hh
                        for kb in range(2):
                            S.op("pe", lambda e, o=bkb[:, hh * 256 + kb * 128:hh * 256 + kb * 128 + 128], i=pS[:, h, kb * 128:(kb + 1) * 128]:
                                 e.transpose(o, i, identb[:, :]), reads=[b_p, b_c2], writes=[bb])
                    S.op("act", lambda e, o=pTt[:, q4 * 4:q4 * 4 + 4, :], i=bkb.rearrange("p (h k) -> p h k", h=4): e.copy(o, i),
                         reads=[bb], writes=[b_pT])
                    yield
                po = [nb(hold=True), nb(hold=True)]
                for h in range(16):
                    bk, bb = po[h // 8]
                    kv = h // 8
                    o = bk[:, (h % 8) * 64:(h % 8) * 64 + 64]
                    S.op("pe", lambda e, o=o, l=pTt[:, h, 0:128], r=vtok[:, t, kv * 64:(kv + 1) * 64]:
                         e.matmul(o, lhsT=l, rhs=r, start=True, stop=False), reads=[b_pT, b_v[t]], writes=[bb])
                    S.op("pe", lambda e, o=o, l=pTt[:, h, 128:256], r=vtok[:, t + 1, kv * 64:(kv + 1) * 64]:
                         e.matmul(o, lhsT=l, rhs=r, start=False, stop=True), reads=[b_pT, b_v[t + 1]], writes=[bb])
                yield
                for hf in range(2):
                    S.op("dve", lambda e, o=OA[:, hf * 512:(hf + 1) * 512].rearrange("p (h d) -> p h d", h=8),
                         i=po[hf][0][:, :].rearrange("p (h d) -> p h d", h=8),
                         s=rden[:, hf * 8:hf * 8 + 8].unsqueeze(2).to_broadcast([128, 8, 64]):
                         e.tensor_tensor(o, i, s, ALU.mult), reads=[po[hf][1], b_sm], writes=[b_OA])
                rel(*po)
                S.op("act", lambda e: e.activation(junk[:, 0:1024], OA, AF.Square, accum_out=gss[:, 12:13]), reads=[b_OA], writes=[b_junk, b_gss])
                S.op("act", lambda e: e.activation(gss[:, 13:14], gss[:, 12:13], AF.Ln, bias=eps_ap, scale=1.0 / 1024),
                     reads=[b_gss, b_c2], writes=[b_gss])
                S.op("act", lambda e: e.activation(gss[:, 14:15], gss[:, 13:14], AF.Exp, scale=-0.5), reads=[b_gss], writes=[b_gss])
                S.op("act", lambda e, o=yti[:, 1024:2048]: e.activation(o, OA, AF.Copy, scale=gss[:, 14:15]),
                     reads=[b_OA, b_gss], writes=[b_yt[t % 2][1]])
            yield

        def ytrans(t):
            tc = slice(t * 128, (t + 1) * 128)
            yti = ytile[t % 2]
            if own:
                chk(8)
                for half in range(2):
                    bk, bb = nb()
                    bkb = bk[:, :].bitcast(BF16)
                    for kc in range(8):
                        c = half * 8 + kc
                        S.op("pe", lambda e, o=bkb[:, kc * 128:(kc + 1) * 128], i=yti[:, c * 128:(c + 1) * 128]: e.transpose(o, i, identb[:, :]),
                             reads=[b_yt[t % 2][half], b_c2], writes=[bb])
                    S.op("dve", lambda e, o=hy[:, half * 8:(half + 1) * 8, tc], i=bkb.rearrange("p (a b) -> p a b", a=8),
                         g=prm[:, P_GC + half * 8:P_GC + (half + 1) * 8].unsqueeze(2).to_broadcast([128, 8, 128]):
                         e.tensor_tensor(o, i, g, ALU.mult), reads=[bb, b_prm], writes=[b_hy[t]])

        def qproj_gen():
            for (col0, cbase) in ((3088, 0), (3600, 4)):
                W, bW = load_w([(0, 512, wblk(w_in, WIN_BLK[col0]))])
                for cl in range(4):
                    c = cbase + cl
                    bk, bb = nb(hold=True)
                    for kc in range(16):
                        S.op("pe", lambda e, o=bk[:, :], l=W[:, kc, cl * 128:(cl + 1) * 128], r=hy[:, kc, :], kc=kc:
                             e.matmul(o, lhsT=l, rhs=r, start=(kc == 0), stop=(kc == 15)), reads=[bW[kc // 8]] + b_hy, writes=[bb])
                        if kc % 4 == 3:
                            yield
                    S.op("act", lambda e, o=qT[:, c, :], i=bk[:, :]: e.copy(o, i), reads=[bb], writes=[b_qT[c]])
                    rel((bk, bb))
                    yield

        def ssd_all():
            for t in range(4):
                yield from ssd_gen(t)

        if own:
            zg = zproj_gen()
            run_il([ssd_gen(0), zg, qproj_gen()], producers={"need_z": zg})
            for t in range(4):
                run_il([attn_gen(t)] + ([ssd_gen(t + 1)] if t < 3 else []))
                ytrans(t)
        else:
            run_il([ssd_all()] + ([next_m1] if next_m1 is not None else []))
        if own or last_prefix:
            for i in range(4):
                S.op("dve", lambda e, i=i: e.tensor_copy(KP[i][:, 0:128], KP[i][:, 512:640]), reads=[b_KP[4]], writes=[b_KP[0]])
            S.op("dve", lambda e: e.tensor_copy(vtok[:, 0, :], vtok[:, 4, :]), reads=[b_v[4]], writes=[b_v[0]])

    def outproj_mlp(xsrc, osrc, next_xsrc=None):
        chk(9)
        mix_alias = b_xbc + b_xtok + b_btok + b_cv
        fence(mix_alias, b_x1)
        for t in range(4):
            S.dma("sp", "x1_%d" % t, lambda e, o=x1v[:, t, :], s=xsrc[t * 128:(t + 1) * 128, :]: e.dma_start(out=o, in_=s), writes=[b_x1[t]])
        for db in range(4):
            W, bW = load_w([(0, 512, wblk(w_out, db))])
            for t in range(4):
                bk, bb = nb()
                for kc in range(16):
                    S.op("pe", lambda e, o=bk[:, :], l=hy[:, kc, t * 128:(t + 1) * 128], r=W[:, kc, 0:512], kc=kc:
                         e.matmul(o, lhsT=l, rhs=r, start=(kc == 0), stop=(kc == 15)), reads=[bW[kc // 8], b_hy[t]], writes=[bb])
                S.op("dve", lambda e, o=x1v[:, t, db * 512:(db + 1) * 512], i=bk[:, :]: e.tensor_tensor(o, o, i, ALU.add),
                     reads=[bb, b_x1[t]], writes=[b_x1[t]])
        chk(10)
        fence(mixer_bufs, mlp_bufs)
        for t in range(4):
            rms_T(x1v[:, t, :], b_x1[t], P_G2, h2T, b_h2, t, xn2, b_xn2, ss2, b_ss2)
        chk(11)
        for fb in range(16):
            W, bW = load_w([(0, 512, wblk(w_up, fb))])
            for sub in range(4):
                fc = fb * 4 + sub
                bk, bb = nb()
                for kc in range(16):
                    S.op("pe", lambda e, o=bk[:, :], l=W[:, kc, sub * 128:(sub + 1) * 128], r=h2T[:, kc, :], kc=kc:
                         e.matmul(o, lhsT=l, rhs=r, start=(kc == 0), stop=(kc == 15)), reads=[bW[kc // 8]] + b_h2, writes=[bb])
                r_, br_ = rl[fc % 2], b_rl[fc % 2]
                S.op("act", lambda e, o=r_, i=bk[:, :]: e.activation(o, i, AF.Relu), reads=[bb], writes=[br_])
                S.op("dve", lambda e, o=aT[:, fc, :], i=r_: e.tensor_tensor(o, i, i, ALU.mult), reads=[br_], writes=[b_aT[fc]])
        chk(12)

        def down_gen():
            for db in range(4):
                acc = [nb(hold=True) for _ in range(4)]
                for fblk in range(4):
                    W, bW = load_w([(0, 512, wblk(w_down, db * 4 + fblk))])
                    for t in range(4):
                        bk, bb = acc[t]
                        for kc in range(16):
                            fc = fblk * 16 + kc
                            S.op("pe", lambda e, o=bk[:, :], l=aT[:, fc, t * 128:(t + 1) * 128], r=W[:, kc, 0:512], fc=fc:
                                 e.matmul(o, lhsT=l, rhs=r, start=(fc == 0), stop=(fc == 63)), reads=[bW[kc // 8], b_aT[fc]], writes=[bb])
                            if kc % 8 == 7:
                                yield
                for t in range(4):
                    bk, bb = acc[t]
                    S.op("dve", lambda e, o=x1v[:, t, db * 512:(db + 1) * 512], i=bk[:, :]: e.tensor_tensor(o, o, i, ALU.add),
                         reads=[bb, b_x1[t]], writes=[b_x1[t]])
                rel(*acc)
                yield

        gens = [down_gen()]
        if next_xsrc is not None:
            fence(b_h2, [b_xth])
            gens.append(m1_gen(next_xsrc, [xth], [b_xth], xn2, b_xn2, ss2, b_ss2, "xth"))
        run_il(gens)
        chk(13)
        outs = []
        for t in range(4):
            S.op("act", lambda e, i=x1v[:, t, :]: e.activation(xn2[:, :], i, AF.Square, accum_out=ss2[:, 0:1]),
                 reads=[b_x1[t]], writes=[b_xn2, b_ss2])
            S.op("act", lambda e: e.activation(ss2[:, 1:2], ss2[:, 0:1], AF.Ln, bias=eps_ap, scale=1.0 / 2048), reads=[b_ss2, b_c2], writes=[b_ss2])
            S.op("act", lambda e: e.activation(ss2[:, 2:3], ss2[:, 1:2], AF.Exp, scale=-0.5), reads=[b_ss2], writes=[b_ss2])
            S.op("dve", lambda e, i=x1v[:, t, :]: e.scalar_tensor_tensor(ot, i, ss2[:, 2:3], fgr[:, :], ALU.mult, ALU.mult),
                 reads=[b_x1[t], b_ss2, b_fgr], writes=b_h2)
            d = S.dma("sp", "out", lambda e, o=osrc[t * 128:(t + 1) * 128, :]: e.dma_start(out=o, in_=ot), reads=b_h2)
            outs.append(d)
        fence(mlp_bufs, mixer_bufs)
        fence(b_x1, mix_alias)
        return outs

    all_out = []
    try:
        srcs = [(xp[pb * 512:(pb + 1) * 512, :], "prefix_last" if pb == n_prefix_blocks - 1 else "prefix") for pb in range(n_prefix_blocks)]
        srcs += [(xo[ob * 512:(ob + 1) * 512, :], "own") for ob in range(n_own_blocks)]
        hoisted = False
        nown = 0
        for bi, (src, kind) in enumerate(srcs):
            nxt = srcs[bi + 1][0] if bi + 1 < len(srcs) else None
            if kind != "own":
                mixer_block(src, kind, False, skip_m1=hoisted, next_m1=(std_m1(nxt) if nxt is not None else None))
                hoisted = nxt is not None
                if kind == "prefix_last":
                    S.op("dve", lambda e: e.tensor_scalar(Sst[:, :], Sst[:, :], flg[:, 0:1], None, ALU.mult), reads=[b_S, b_flg], writes=[b_S])
                    S.op("dve", lambda e: e.tensor_copy(Sbf[:, :], Sst[:, :]), reads=[b_S], writes=[b_Sbf])
            else:
                mixer_block(src, kind, nown == 0, skip_m1=hoisted)
                all_out += outproj_mlp(src, out_d[nown * 512:(nown + 1) * 512, :], next_xsrc=nxt)
                hoisted = nxt is not None
                nown += 1
    except _Stop:
        pass
    if dumps:
        allb = [b_prm, b_cst, b_c2, b_S, b_Sbf] + b_tail + b_KP + b_v + b_W + b_hy + b_x1 + b_xbc + b_xtok + b_btok + b_cv + mixer_bufs + mlp_bufs
        loc = dict(hy=hy, X1=X1, AR=AR, Sst=Sst, Sbf=Sbf, KP0=KP[0], KP1=KP[1], KP2=KP[2], KP3=KP[3], vtok=vtok, xBCT=xBCT, xtok=xtok, Btok=Btok,
                   sz=sz, qT=qT, dtt=dtt, att=att, ex=ex, LT2=LT2, MT=MT, Yt=Yt, yt0=ytile[0], yt1=ytile[1], OA=OA, pS=pS, h2T=h2T, aT=aT, tailu=tailu,
                   rden=rden, mrow=mrow, rsum=rsum, gss=gss, Xd=Xd, Xdd=Xdd, pTt=pTt)
        for name in dumps:
            t_ = loc[name]
            src = t_ if isinstance(t_, bass.AP) else t_.ap()
            shp = list(src.shape)
            d_ = nc.dram_tensor("dbg_" + name, shp, src.dtype, kind="ExternalOutput").ap()
            all_out.append(S.dma("sp", "dbg_" + name, lambda e, o=d_, s_=src: e.dma_start(out=o, in_=s_), reads=allb))
    S.barrier_wait("sp", all_out)
    emit(nc, S)
    return nc


def _pack_params(inp):
    f = np.float32
    prm = np.zeros((128, NPRM), f)
    prm[:, P_G1:P_G1 + 16] = inp["mix_norm_g"][0].reshape(16, 128).T
    prm[:, P_G2:P_G2 + 16] = inp["mlp_norm_g"][0].reshape(16, 128).T
    prm[:, P_GC:P_GC + 8] = inp["ssm_norm_g"][0].reshape(8, 128).T
    prm[:, P_GC + 8:P_GC + 16] = inp["attn_out_norm_g"][0].reshape(8, 128).T
    cw = inp["conv_w"][0]
    prm[:, P_CW:P_CW + 64] = cw.reshape(4, 16, 128).transpose(2, 1, 0).reshape(128, 64)
    prm[:, P_CB:P_CB + 16] = inp["conv_b"][0].reshape(16, 128).T
    prm[:, P_DTB:P_DTB + 16] = np.broadcast_to(inp["dt_bias"][0], (128, 16))
    prm[:, P_ALOG:P_ALOG + 16] = np.broadcast_to(inp["A_log"][0], (128, 16))
    prm[:, P_DSK:P_DSK + 16] = np.broadcast_to(inp["D_skip"][0], (128, 16))
    prm[:, P_SNK:P_SNK + 16] = np.broadcast_to(inp["attn_sinks"][0], (128, 16))
    return prm


def _consts(first_half):
    f = np.float32
    cst = np.zeros((128, NCST), f)
    i = np.arange(128)
    cst[:, C_ID:C_ID + 128] = np.eye(128, dtype=f)
    cst[:, C_TRI:C_TRI + 128] = (i[:, None] <= i[None, :]).astype(f)
    cst[:, C_US:C_US + 128] = (i[:, None] > i[None, :]).astype(f)
    m = np.where(i[None, :] >= i[:, None], 0.0, NEG).astype(f)
    cst[:, C_MR4:C_MR4 + 512] = np.tile(m, (1, 4))
    j = np.arange(256)
    diff = 128 + i[:, None] - j[None, :]
    am = np.where((diff >= 0) & (diff < 128), 0.0, NEG).astype(f)
    cst[:, C_AM:C_AM + 256] = am
    am0 = am.copy()
    if first_half:
        am0[:, 0:128] = NEG
    cst[:, C_AM0:C_AM0 + 256] = am0
    return cst


_NC_CACHE = {}


def _blk(W, r0, c0, ncols=512):
    return np.ascontiguousarray(W[r0:r0 + 2048, c0:c0 + ncols].reshape(16, 128, ncols).transpose(1, 0, 2)).reshape(128, 16 * ncols)


def make_in_maps(inputs, cores=range(8)):
    inp = {k: np.asarray(v) for k, v in inputs.items()}
    x = np.ascontiguousarray(inp["x"], dtype=np.float32)
    prm = _pack_params(inp)
    fgr = np.ascontiguousarray(np.broadcast_to(inp["final_norm_g"].astype(np.float32), (128, 2048)))
    w_in = np.asarray(inp["w_in"][0], dtype=np.float32)
    w_out = np.asarray(inp["w_out"][0], dtype=np.float32)
    w_up = np.asarray(inp["w_up"][0], dtype=np.float32)
    w_down = np.asarray(inp["w_down"][0], dtype=np.float32)
    w_in_p = np.stack([_blk(w_in, 0, c0) for c0 in (1024, 1536, 2048, 2560, 3088, 3600, 0, 512)])
    w_kv = _blk(w_in, 0, 4112, 256)
    w_out_p = np.stack([_blk(w_out, 0, db * 512) for db in range(4)])
    w_up_p = np.stack([_blk(w_up, 0, fb * 512) for fb in range(16)])
    w_down_p = np.stack([_blk(w_down, fblk * 2048, db * 512) for db in range(4) for fblk in range(4)])
    wdt = np.ascontiguousarray(w_in[:, 3072:3088].reshape(16, 128, 16).transpose(1, 0, 2).reshape(128, 256))
    in_maps = []
    for c in cores:
        b, half = c // 2, c % 2
        xo = x[b, half * 1024:(half + 1) * 1024]
        xp = x[b, 0:1024] if half == 1 else np.zeros((1024, 2048), np.float32)
        in_maps.append({
            "xo": np.ascontiguousarray(xo), "xp": np.ascontiguousarray(xp),
            "w_in": w_in_p, "w_kv": w_kv, "w_out": w_out_p, "w_up": w_up_p, "w_down": w_down_p,
            "prm": prm, "cst": _consts(half == 0), "fgr": fgr,
            "flg": np.full((128, 1), float(half), np.float32),
            "wdt": wdt,
        })
    return in_maps


def kernel(**inputs):
    if "nc" not in _NC_CACHE:
        _NC_CACHE["nc"] = build_nc()
    nc = _NC_CACHE["nc"]
    in_maps = make_in_maps(inputs)
    res = run_bass_kernel_spmd(nc, in_maps, core_ids=list(range(8)))
    out = np.zeros((4, 2048, 2048), np.float32)
    for c in range(8):
        b, half = c // 2, c % 2
        out[b, half * 1024:(half + 1) * 1024] = res.results[c]["out"]
    return out
```

```python
import numpy as np
import concourse.bass as bass
import concourse.mybir as mybir
from concourse.bass_utils import run_bass_kernel_spmd

F32 = mybir.dt.float32
BF16 = mybir.dt.bfloat16
AF = mybir.ActivationFunctionType
ALU = mybir.AluOpType
AX = mybir.AxisListType

ENGS = ("pe", "dve", "act", "pool", "sp")
EPS = 1e-5
NEG = -30000.0


class Buf:
    __slots__ = ("name", "lw", "rd")

    def __init__(self, name):
        self.name = name
        self.lw = None
        self.rd = []


class Ins:
    __slots__ = ("eng", "fn", "deps", "is_dma", "key", "signal", "rank", "idx")

    def __init__(self, eng, fn, is_dma=False, key=None):
        self.eng = eng
        self.fn = fn
        self.deps = []
        self.is_dma = is_dma
        self.key = key
        self.signal = False
        self.rank = 0
        self.idx = 0


class Sched:
    def __init__(self, nc):
        self.nc = nc
        self.streams = {e: [] for e in ENGS}
        self.keys = {}

    def _deps(self, ins, reads, writes):
        deps = {}
        raw = set()
        for b in reads:
            if b.lw is not None:
                deps[id(b.lw)] = b.lw
                raw.add(id(b.lw))
        for b in writes:
            if b.lw is not None:
                deps[id(b.lw)] = b.lw
            for r in b.rd:
                deps[id(r)] = r
        best = {}
        out = []
        for k, d in deps.items():
            if d is ins:
                continue
            if d.is_dma:
                out.append(d)
                continue
            if (not ins.is_dma) and d.eng == ins.eng:
                if ins.eng == "pe":
                    continue
            cur = best.get(d.eng)
            if cur is None or cur.idx < d.idx:
                best[d.eng] = d
        out.extend(best.values())
        ins.deps = out
        for b in writes:
            b.lw = ins
            b.rd = []
        for b in reads:
            if b.lw is not ins:
                b.rd.append(ins)

    def op(self, eng, fn, reads=(), writes=()):
        ins = Ins(eng, fn)
        ins.idx = len(self.streams[eng])
        self._deps(ins, reads, writes)
        self.streams[eng].append(ins)
        return ins

    def dma(self, eng, key, fn, reads=(), writes=()):
        ins = Ins(eng, fn, is_dma=True, key=key)
        ins.idx = len(self.streams[eng])
        self._deps(ins, reads, writes)
        lst = self.keys.setdefault(key, [])
        lst.append(ins)
        ins.rank = 16 * len(lst)
        ins.signal = True
        self.streams[eng].append(ins)
        return ins

    def barrier_wait(self, eng, ins_list):
        ins = Ins(eng, None)
        ins.deps = list(ins_list)
        ins.idx = len(self.streams[eng])
        self.streams[eng].append(ins)
        return ins


def emit(nc, sched):
    import contextlib

    with contextlib.ExitStack() as st:
        esem = {e: st.enter_context(nc.semaphore("s_" + e)) for e in ENGS}
        ksem = {k: st.enter_context(nc.semaphore("k_%d" % i)) for i, k in enumerate(sched.keys)}
        for e in ENGS:
            for ins in sched.streams[e]:
                for d in ins.deps:
                    d.signal = True
        for e in ENGS:
            r = 0
            for ins in sched.streams[e]:
                if ins.is_dma:
                    continue
                if ins.signal:
                    r += 1
                    ins.rank = r
        block = st.enter_context(nc.Block())

        def run(e, engobj):
            seen = {}
            for ins in sched.streams[e]:
                need = {}
                for d in ins.deps:
                    sem = ksem[d.key] if d.is_dma else esem[d.eng]
                    kk = id(sem)
                    if kk not in need or need[kk][1] < d.rank:
                        need[kk] = (sem, d.rank)
                for kk, (sem, val) in need.items():
                    if seen.get(kk, 0) >= val:
                        continue
                    engobj.wait_ge(sem, val)
                    seen[kk] = val
                if ins.fn is None:
                    continue
                bi = ins.fn(engobj)
                if ins.is_dma:
                    bi.then_inc(ksem[ins.key], 16)
                elif ins.signal:
                    bi.then_inc(esem[e], 1)

        @block.tensor
        def _(eng):
            run("pe", eng)

        @block.vector
        def _(eng):
            run("dve", eng)

        @block.scalar
        def _(eng):
            run("act", eng)

        @block.gpsimd
        def _(eng):
            run("pool", eng)

        @block.sync
        def _(eng):
            run("sp", eng)


P_G1, P_G2, P_GC, P_CW, P_CB, P_DTB, P_ALOG, P_DSK, P_SNK, NPRM = 0, 16, 32, 48, 112, 128, 144, 160, 176, 192
C_ID, C_TRI, C_US, C_MR4, C_AM, C_AM0, NCST = 0, 128, 256, 384, 896, 1152, 1408


class _Stop(Exception):
    pass


def build_nc(n_prefix_blocks=2, n_own_blocks=2, stage=99, dumps=()):
    nc = bass.Bass("TRN2", target_bir_lowering=False)
    S = Sched(nc)

    def chk(st):
        if stage <= st:
            raise _Stop()

    def dram(name, shape, kind="ExternalInput"):
        return nc.dram_tensor(name, shape, F32, kind=kind).ap()

    xo = dram("xo", [1024, 2048])
    xp = dram("xp", [1024, 2048])
    w_in = dram("w_in", [8, 128, 8192])
    w_kv = dram("w_kv", [128, 4096])
    w_out = dram("w_out", [4, 128, 8192])
    w_up = dram("w_up", [16, 128, 8192])
    w_down = dram("w_down", [16, 128, 8192])
    prm_d = dram("prm", [128, NPRM])
    cst_d = dram("cst", [128, NCST])
    fgr_d = dram("fgr", [128, 2048])
    flg_d = dram("flg", [128, 1])
    wdt_d = dram("wdt", [128, 256])
    out_d = dram("out", [1024, 2048], kind="ExternalOutput")

    def wblk(t, i):
        return t[i].rearrange("p (k c) -> p k c", k=16)

    WIN_BLK = {1024: 0, 1536: 1, 2048: 2, 2560: 3, 3088: 4, 3600: 5, 0: 6, 512: 7}

    def sb(name, shape, dt=F32):
        return nc.alloc_sbuf_tensor(name, shape, dt)

    prm = sb("prm_s", [128, NPRM]); b_prm = Buf("prm")
    cst = sb("cst_s", [128, NCST]); b_cst = Buf("cst")
    fgr = sb("fgr_s", [128, 2048]); b_fgr = Buf("fgr")
    flg = sb("flg_s", [128, 1]); b_flg = Buf("flg")
    identb = sb("identb", [128, 128], BF16)
    mr4b = sb("mr4b", [128, 512], BF16)
    amb = sb("amb", [128, 256], BF16)
    am0b = sb("am0b", [128, 256], BF16)
    ones = sb("ones", [128, 128])
    cvals = sb("cvals", [128, 4])
    aneg = sb("aneg", [128, 16])
    b_c2 = Buf("consts2")
    wdtf = sb("wdtf", [128, 256]); b_wdtf = Buf("wdtf")
    wdtb = sb("wdtb", [128, 16, 16], BF16)
    Sst = sb("Sst", [128, 1024]); b_S = Buf("S")
    Sbf = sb("Sbf", [128, 1024], BF16); b_Sbf = Buf("Sbf")
    tailu = sb("tailu", [128, 16, 3]); b_tail = [Buf("tail%d" % c) for c in range(16)]
    KP = [sb("KP%d" % i, [128, 5 * 128], BF16) for i in range(4)]
    b_KP = [Buf("KPs%d" % s) for s in range(5)]
    vtok = sb("vtok", [128, 5, 128], BF16)
    b_v = [Buf("v%d" % s) for s in range(5)]

    NW = 4
    Wt = [sb("W%d" % i, [128, 8, 512], BF16) for i in range(NW)]
    b_W = [Buf("W%d" % i) for i in range(NW)]
    wctr = [0]
    first_w_dep = []

    class WBlock:
        def __init__(self, idx):
            self.idx = idx

        def __getitem__(self, key):
            _, kc, cs = key
            return Wt[self.idx[kc // 8]][:, kc % 8, cs]

    def load_w(parts):
        (lo, hi, src), = parts
        idx = []
        for h in range(2):
            i = wctr[0] % NW
            wctr[0] += 1
            idx.append(i)
            S.dma("pool", "w%d" % i, lambda e, o=Wt[i][:, :, lo:hi], s=src[:, h * 8:(h + 1) * 8, :]: e.dma_start(out=o, in_=s),
                  reads=(first_w_dep.pop() if first_w_dep else []), writes=[b_W[i]])
        return WBlock(idx), [b_W[idx[0]], b_W[idx[1]]]

    hy = sb("hy", [128, 16, 512], BF16)
    b_hy = [Buf("hy%d" % t) for t in range(4)]
    X1 = sb("X1", [128, 8192])
    x1v = X1[:, :].rearrange("p (t c) -> p t c", t=4)
    b_x1 = [Buf("x1_%d" % t) for t in range(4)]
    xBCT = X1[:, 0:4096].bitcast(BF16).rearrange("p (c n) -> p c n", c=16)
    b_xbc = [Buf("xbc%d" % c) for c in range(16)]
    xtok = X1[:, 4096:6144].bitcast(BF16).rearrange("p (t c) -> p t c", t=4)
    b_xtok = [Buf("xtok%d" % t) for t in range(4)]
    Btok = X1[:, 6144:7168].bitcast(BF16).rearrange("p (t c) -> p t c", t=4)
    b_btok = [Buf("btok%d" % t) for t in range(4)]
    cvb = [X1[:, 7168:7680], X1[:, 7680:8192]]
    b_cv = [Buf("cv0"), Buf("cv1")]
    fence_d = sb("fence_d", [128, 4]); b_fence = Buf("fence")

    def fence(old, new):
        S.op("dve", lambda e: e.memset(fence_d[:, 0:1], 0.0), writes=list(old) + list(new) + [b_fence])

    AR = sb("AR", [128, 22528])
    off = [0]

    def carve(nwords):
        a = off[0]
        off[0] += nwords
        assert off[0] <= 22528, off[0]
        return AR[:, a:a + nwords]

    sz = carve(2048).bitcast(BF16).rearrange("p (t c) -> p t c", t=4)
    b_sz = [Buf("sz%d" % t) for t in range(4)]
    qT = carve(2048).bitcast(BF16).rearrange("p (c n) -> p c n", c=8)
    b_qT = [Buf("qT%d" % c) for c in range(8)]
    ytile = [carve(1024).bitcast(BF16) for _ in range(2)]
    b_yt = [[Buf("yts%d" % i), Buf("yta%d" % i)] for i in range(2)]
    dtt = carve(64).rearrange("p (t h) -> p t h", t=4); b_dt = [Buf("dt%d" % t) for t in range(4)]
    att = carve(64).rearrange("p (t h) -> p t h", t=4); b_a = [Buf("a%d" % t) for t in range(4)]
    xt0_ = carve(2048)
    xn = carve(1024).bitcast(BF16); b_xn = Buf("xn")
    junk = carve(512).bitcast(BF16); b_junk = Buf("junk")
    ss = carve(8); b_ss = Buf("ss")
    ub = [carve(516), carve(516)]; b_u = [Buf("u0"), Buf("u1")]
    ex = carve(48); b_ex = Buf("ex")
    sp_tmp = carve(16); b_sp = Buf("sp_tmp")
    aTri2 = carve(2048); aTri = aTri2.rearrange("p (h l) -> p h l", h=16); b_aTri = Buf("aTri")
    LT2 = carve(1024).bitcast(BF16); LT = LT2.rearrange("p (h l) -> p h l", h=16); b_LT = Buf("LT")
    MT = carve(1024).bitcast(BF16).rearrange("p (h l) -> p h l", h=16); b_MT = Buf("MT")
    Xd = carve(512).bitcast(BF16); b_Xd = Buf("Xd")
    Xdd = carve(512).bitcast(BF16); b_Xdd = Buf("Xdd")
    Yt = carve(1024); b_Y = Buf("Y")
    gss = carve(16); b_gss = Buf("gss")
    pS_reg = carve(2048); b_p = Buf("p")
    pS = pS_reg.bitcast(BF16).rearrange("p (h k) -> p h k", h=16)
    xt = [xt0_, pS_reg]; b_xt = [Buf("xt0"), b_p]
    pTt = carve(2048).bitcast(BF16).rearrange("p (h k) -> p h k", h=16); b_pT = Buf("pT")
    rmax = carve(16); negm = carve(16); rsum = carve(16); mrow = carve(16); esk = carve(16); rden = carve(16)
    b_sm = Buf("softmax_small")
    OA = carve(1024); b_OA = Buf("OA")
    XDk = aTri2[:, 1024:2048]; b_XDk = b_aTri
    mixer_top = off[0]
    mixer_bufs = (b_sz + b_qT + [b for pr in b_yt for b in pr] + b_dt + b_a + [b_xt[0], b_xn, b_junk, b_ss] + b_u +
                  [b_ex, b_sp, b_aTri, b_LT, b_MT, b_Xd, b_Xdd, b_Y, b_gss, b_p, b_pT, b_sm, b_OA])
    off[0] = 0
    h2_reg = carve(4096)
    h2T = h2_reg.bitcast(BF16).rearrange("p (c n) -> p c n", c=16)
    b_h2 = [Buf("h2_%d" % t) for t in range(4)]
    aT = carve(16384).bitcast(BF16).rearrange("p (c n) -> p c n", c=64)
    b_aT = [Buf("aT%d" % c) for c in range(64)]
    ot = h2_reg[:, 0:2048]
    xth = h2_reg[:, 2048:4096]; b_xth = Buf("xth")
    xn2 = carve(1024).bitcast(BF16); b_xn2 = Buf("xn2")
    ss2 = carve(8); b_ss2 = Buf("ss2")
    rl = [carve(256).bitcast(BF16), carve(256).bitcast(BF16)]; b_rl = [Buf("rl0"), Buf("rl1")]
    mlp_bufs = b_h2 + b_aT + [b_xn2, b_ss2, b_xth] + b_rl

    banks = [nc.alloc_psum_tensor("bank%d" % i, [128, 512], F32) for i in range(8)]
    b_bank = [Buf("bank%d" % i) for i in range(8)]
    bctr = [0]

    reserved = set()

    def nb(hold=False):
        for _ in range(17):
            i = bctr[0] % 8
            bctr[0] += 1
            if i not in reserved:
                break
        else:
            raise RuntimeError("all PSUM banks reserved")
        if hold:
            reserved.add(i)
        return banks[i], b_bank[i]

    def rel(*pairs):
        for (_, bb) in pairs:
            reserved.discard(b_bank.index(bb))

    S.dma("sp", "c0", lambda e: e.dma_start(out=prm[:, :], in_=prm_d[:, :]), writes=[b_prm])
    S.dma("sp", "c1", lambda e: e.dma_start(out=cst[:, :], in_=cst_d[:, :]), writes=[b_cst])
    S.dma("sp", "c2", lambda e: e.dma_start(out=fgr[:, :], in_=fgr_d[:, :]), writes=[b_fgr])
    S.dma("sp", "c3", lambda e: e.dma_start(out=flg[:, :], in_=flg_d[:, :]), writes=[b_flg])
    S.dma("sp", "c4", lambda e: e.dma_start(out=wdtf[:, :], in_=wdt_d[:, :]), writes=[b_wdtf])
    S.op("dve", lambda e: e.tensor_copy(wdtb[:, :, :], wdtf[:, :].rearrange("p (k c) -> p k c", k=16)), reads=[b_wdtf], writes=[b_c2])
    S.op("dve", lambda e: e.tensor_copy(identb[:, :], cst[:, C_ID:C_ID + 128]), reads=[b_cst], writes=[b_c2])
    S.op("dve", lambda e: e.tensor_copy(mr4b[:, :], cst[:, C_MR4:C_MR4 + 512]), reads=[b_cst], writes=[b_c2])
    S.op("dve", lambda e: e.tensor_copy(amb[:, :], cst[:, C_AM:C_AM + 256]), reads=[b_cst], writes=[b_c2])
    S.op("dve", lambda e: e.tensor_copy(am0b[:, :], cst[:, C_AM0:C_AM0 + 256]), reads=[b_cst], writes=[b_c2])
    S.op("dve", lambda e: e.memset(ones[:, :], 1.0), writes=[b_c2])
    S.op("dve", lambda e: e.memset(cvals[:, 0:1], EPS), writes=[b_c2])
    S.op("dve", lambda e: e.memset(cvals[:, 1:2], 1.0), writes=[b_c2])
    S.op("dve", lambda e: e.memset(cvals[:, 2:3], 0.0), writes=[b_c2])
    S.op("act", lambda e: e.activation(aneg[:, :], prm[:, P_ALOG:P_ALOG + 16], AF.Exp), reads=[b_prm], writes=[b_c2])
    S.op("dve", lambda e: e.tensor_scalar(aneg[:, :], aneg[:, :], -1.0, None, ALU.mult), reads=[b_c2], writes=[b_c2])
    S.op("dve", lambda e: e.memset(Sst[:, :], 0.0), writes=[b_S])
    S.op("dve", lambda e: e.memset(Sbf[:, :], 0.0), writes=[b_Sbf])
    S.op("dve", lambda e: e.memset(tailu[:, :, :], 0.0), writes=b_tail)
    for i in range(4):
        S.op("dve", lambda e, i=i: e.memset(KP[i][:, :], 0.0), writes=b_KP)
    S.op("dve", lambda e: e.memset(vtok[:, :, :], 0.0), writes=b_v)
    identf = cst[:, C_ID:C_ID + 128]
    tri = cst[:, C_TRI:C_TRI + 128]
    ustr = cst[:, C_US:C_US + 128]
    eps_ap = cvals[:, 0:1]
    one_ap = cvals[:, 1:2]

    def rms_T(*a):
        for _ in rms_T_gen(*a):
            pass

    def rms_T_gen(xin, xin_buf, gcol, dst, dst_bufs, t, xn_, b_xn_, ss_, b_ss_):
        S.op("act", lambda e: e.activation(xn_[:, :], xin, AF.Square, accum_out=ss_[:, 0:1]),
             reads=[xin_buf], writes=[b_xn_, b_ss_])
        S.op("act", lambda e: e.activation(ss_[:, 1:2], ss_[:, 0:1], AF.Ln, bias=eps_ap, scale=1.0 / 2048),
             reads=[b_ss_, b_c2], writes=[b_ss_])
        S.op("act", lambda e: e.activation(ss_[:, 2:3], ss_[:, 1:2], AF.Exp, scale=-0.5), reads=[b_ss_], writes=[b_ss_])
        S.op("act", lambda e: e.activation(xn_[:, :], xin, AF.Copy, scale=ss_[:, 2:3]),
             reads=[xin_buf, b_ss_], writes=[b_xn_])
        yield
        for half in range(2):
            bk, bb = nb(hold=True)
            bkb = bk[:, :].bitcast(BF16)
            for kc in range(8):
                c = half * 8 + kc
                S.op("pe", lambda e, o=bkb[:, kc * 128:(kc + 1) * 128], i=xn_[:, c * 128:(c + 1) * 128]:
                     e.transpose(o, i, identb[:, :]), reads=[b_xn_, b_c2], writes=[bb])
                if kc % 4 == 3:
                    yield
            S.op("dve", lambda e, o=dst[:, half * 8:(half + 1) * 8, t * 128:(t + 1) * 128],
                 i=bkb.rearrange("p (a b) -> p a b", a=8),
                 g=prm[:, gcol + half * 8:gcol + (half + 1) * 8].unsqueeze(2).to_broadcast([128, 8, 128]):
                 e.tensor_tensor(o, i, g, ALU.mult), reads=[bb, b_prm], writes=[dst_bufs[t]])
            rel((bk, bb))
            yield

    def run_il(gens):
        gens = list(gens)
        while gens:
            for g_ in list(gens):
                try:
                    next(g_)
                except StopIteration:
                    gens.remove(g_)

    def m1_gen(xsrc, xts, b_xts, xn_, b_xn_, ss_, b_ss_, keyp):
        for t in range(4):
            i = t % len(xts)
            S.dma("sp", keyp + str(i), lambda e, o=xts[i], s_=xsrc[t * 128:(t + 1) * 128, :]: e.dma_start(out=o, in_=s_),
                  writes=[b_xts[i]])
            yield from rms_T_gen(xts[i], b_xts[i], P_G1, hy, b_hy, t, xn_, b_xn_, ss_, b_ss_)

    dbg = {}

    first_w_dep.append([b_xt[0]])

    def std_m1(xsrc):
        return m1_gen(xsrc, [xt[0][:, :], xt[1][:, :]], b_xt, xn, b_xn, ss, b_ss, "xt")

    def mixer_block(xsrc, kind, first_own, skip_m1=False, next_m1=None):
        own = kind == "own"
        last_prefix = kind == "prefix_last"
        if not skip_m1:
            run_il([std_m1(xsrc)])

        chk(1)
        def fm_block(col0, nchunk, evac):
            assert nchunk == 4
            W, bW = load_w([(0, 512, wblk(w_in, WIN_BLK[col0]))])
            for cl in range(nchunk):
                bk, bb = nb()
                for kc in range(16):
                    S.op("pe", lambda e, o=bk[:, :], l=W[:, kc, cl * 128:(cl + 1) * 128], r=hy[:, kc, :], kc=kc:
                         e.matmul(o, lhsT=l, rhs=r, start=(kc == 0), stop=(kc == 15)), reads=[bW[kc // 8]] + b_hy, writes=[bb])
                evac(cl, bk, bb)

        def conv_evac(cbase):
            def f(cl, bk, bb):
                c = cbase + cl
                i = c % 2
                u, bu = ub[i], b_u[i]
                cv, bcv = cvb[i], b_cv[i]
                S.op("dve", lambda e: e.tensor_copy(u[:, 0:3], tailu[:, c, :]), reads=[b_tail[c]], writes=[bu])
                S.op("act", lambda e: e.copy(u[:, 3:515], bk[:, :]), reads=[bb], writes=[bu])
                S.op("dve", lambda e: e.tensor_copy(tailu[:, c, :], u[:, 512:515]), reads=[bu], writes=[b_tail[c]])
                S.op("dve", lambda e: e.tensor_scalar(cv, u[:, 0:512], prm[:, P_CW + c * 4:P_CW + c * 4 + 1],
                                                      prm[:, P_CB + c:P_CB + c + 1], ALU.mult, ALU.add),
                     reads=[bu, b_prm], writes=[bcv])
                for k in range(1, 4):
                    S.op("dve", lambda e, k=k: e.scalar_tensor_tensor(cv, u[:, k:k + 512], prm[:, P_CW + c * 4 + k:P_CW + c * 4 + k + 1],
                                                                      cv, ALU.mult, ALU.add),
                         reads=[bu, b_prm, bcv], writes=[bcv])
                S.op("act", lambda e: e.activation(xBCT[:, c, :], cv, AF.Silu), reads=[bcv], writes=[b_xbc[c]])
            return f

        fm_block(1024, 4, conv_evac(0))
        fm_block(1536, 4, conv_evac(4))
        fm_block(2048, 4, conv_evac(8))
        if own or last_prefix:
            fm_block(2560, 4, conv_evac(12))

        chk(2)
        W, bW = load_w([(0, 256, w_kv.rearrange("p (k c) -> p k c", k=16))])
        if own or last_prefix:
            bk, bb = nb()
            for kc in range(16):
                S.op("pe", lambda e, o=bk[:, :], l=W[:, kc, 0:128], r=hy[:, kc, :], kc=kc:
                     e.matmul(o, lhsT=l, rhs=r, start=(kc == 0), stop=(kc == 15)), reads=[bW[kc // 8]] + b_hy, writes=[bb])
            S.op("act", lambda e, o=KP[0][0:64, 128:640], i=bk[0:64, :]: e.copy(o, i), reads=[bb], writes=b_KP[1:5])
            S.op("act", lambda e, o=KP[3][64:128, 128:640], i=bk[64:128, :]: e.copy(o, i), reads=[bb], writes=b_KP[1:5])
            S.dma("sp", "kp1", lambda e: e.dma_start(out=KP[1][64:128, 128:640], in_=KP[0][0:64, 128:640]), reads=b_KP[1:5], writes=b_KP[1:5])
            S.dma("sp", "kp2", lambda e: e.dma_start(out=KP[2][0:64, 128:640], in_=KP[3][64:128, 128:640]), reads=b_KP[1:5], writes=b_KP[1:5])
        for t in range(4):
            bk, bb = nb()
            if own or last_prefix:
                for kc in range(16):
                    S.op("pe", lambda e, o=bk[:, 0:128], l=hy[:, kc, t * 128:(t + 1) * 128], r=W[:, kc, 128:256], kc=kc:
                         e.matmul(o, lhsT=l, rhs=r, start=(kc == 0), stop=(kc == 15)), reads=[bW[kc // 8], b_hy[t]], writes=[bb])
            for kc in range(16):
                S.op("pe", lambda e, o=bk[:, 128:144], l=hy[:, kc, t * 128:(t + 1) * 128], r=wdtb[:, kc, :], kc=kc:
                     e.matmul(o, lhsT=l, rhs=r, start=(kc == 0), stop=(kc == 15)), reads=[b_c2, b_hy[t]], writes=[bb])
            if own or last_prefix:
                S.op("act", lambda e, o=vtok[:, 1 + t, :], i=bk[:, 0:128]: e.copy(o, i), reads=[bb], writes=[b_v[1 + t]])
            S.op("dve", lambda e, i=bk[:, 128:144]: e.tensor_tensor(sp_tmp[:, :], i, prm[:, P_DTB:P_DTB + 16], ALU.add),
                 reads=[bb, b_prm], writes=[b_sp])
            S.op("act", lambda e: e.activation(sp_tmp[:, :], sp_tmp[:, :], AF.Exp), reads=[b_sp], writes=[b_sp])
            S.op("act", lambda e, o=dtt[:, t, :]: e.activation(o, sp_tmp[:, :], AF.Ln, bias=one_ap), reads=[b_sp, b_c2], writes=[b_dt[t]])
            S.op("dve", lambda e, o=att[:, t, :], i=dtt[:, t, :]: e.tensor_tensor(o, i, aneg[:, :], ALU.mult),
                 reads=[b_dt[t], b_c2], writes=[b_a[t]])

        chk(3)
        if own:
            for zb in range(2):
                W, bW = load_w([(0, 512, wblk(w_in, WIN_BLK[zb * 512]))])
                for t in range(4):
                    bk, bb = nb()
                    for kc in range(16):
                        S.op("pe", lambda e, o=bk[:, :], l=hy[:, kc, t * 128:(t + 1) * 128], r=W[:, kc, 0:512], kc=kc:
                             e.matmul(o, lhsT=l, rhs=r, start=(kc == 0), stop=(kc == 15)), reads=[bW[kc // 8], b_hy[t]], writes=[bb])
                    S.op("act", lambda e, o=sz[:, t, zb * 512:(zb + 1) * 512], i=bk[:, :]: e.activation(o, i, AF.Silu),
                         reads=[bb], writes=[b_sz[t]])

        chk(4)
        def ssd_gen(t):
            tc = slice(t * 128, (t + 1) * 128)
            bk0, bb0 = nb()
            bk0b = bk0[:, :].bitcast(BF16)
            for c in range(8):
                S.op("pe", lambda e, o=bk0b[:, c * 128:(c + 1) * 128], i=xBCT[:, c, tc]: e.transpose(o, i, identb[:, :]),
                     reads=[b_xbc[c], b_c2], writes=[bb0])
            S.op("dve", lambda e, o=xtok[:, t, :], i=bk0b: e.tensor_copy(o, i), reads=[bb0], writes=[b_xtok[t]])
            bk1, bb1 = nb()
            bk1b = bk1[:, :].bitcast(BF16)
            for c in range(4):
                S.op("pe", lambda e, o=bk1b[:, c * 128:(c + 1) * 128], i=xBCT[:, 8 + c, tc]: e.transpose(o, i, identb[:, :]),
                     reads=[b_xbc[8 + c], b_c2], writes=[bb1])
            S.op("dve", lambda e, o=Btok[:, t, :], i=bk1b[:, 0:512]: e.tensor_copy(o, i), reads=[bb1], writes=[b_btok[t]])

            yield
            chk(5)
            bka, bba = nb()
            a_t = att[:, t, :]
            S.op("pe", lambda e, o=bka[:, 0:16]: e.matmul(o, lhsT=tri, rhs=a_t, start=True, stop=True), reads=[b_cst, b_a[t]], writes=[bba])
            S.op("pe", lambda e, o=bka[:, 16:32]: e.matmul(o, lhsT=ones[:, :], rhs=a_t, start=True, stop=True), reads=[b_c2, b_a[t]], writes=[bba])
            S.op("pe", lambda e, o=bka[:, 32:48]: e.matmul(o, lhsT=ustr, rhs=a_t, start=True, stop=True), reads=[b_cst, b_a[t]], writes=[bba])
            S.op("act", lambda e, i=bka[:, 0:48]: e.activation(ex[:, :], i, AF.Exp), reads=[bba], writes=[b_ex])
            eacs_b = ex[:, 0:16].unsqueeze(2).to_broadcast([128, 16, 64])
            cdec_b = ex[:, 16:32].unsqueeze(2).to_broadcast([128, 16, 64])
            dte_b = ex[:, 32:48].unsqueeze(2).to_broadcast([128, 16, 64])
            dt_b = dtt[:, t, :].unsqueeze(2).to_broadcast([128, 16, 64])
            xt3 = xtok[:, t, :].rearrange("p (h d) -> p h d", h=16)
            Xd3 = Xd.rearrange("p (h d) -> p h d", h=16)
            Xdd3 = Xdd.rearrange("p (h d) -> p h d", h=16)
            S.op("dve", lambda e: e.tensor_tensor(Xd3, xt3, dt_b, ALU.mult), reads=[b_xtok[t], b_dt[t]], writes=[b_Xd])
            S.op("dve", lambda e: e.tensor_tensor(Xdd3, Xd3, dte_b, ALU.mult), reads=[b_Xd, b_ex], writes=[b_Xdd])

            yield
            if own:
                S.op("dve", lambda e: e.tensor_tensor(aTri, tri.unsqueeze(1).to_broadcast([128, 16, 128]),
                                                      a_t.unsqueeze(2).to_broadcast([128, 16, 128]), ALU.mult),
                     reads=[b_cst, b_a[t]], writes=[b_aTri])
                for g in range(4):
                    bk, bb = nb()
                    S.op("pe", lambda e, o=bk[:, :], r=aTri2[:, g * 512:(g + 1) * 512]: e.matmul(o, lhsT=ustr, rhs=r, start=True, stop=False),
                         reads=[b_cst, b_aTri], writes=[bb])
                    S.op("pe", lambda e, o=bk[:, :]: e.matmul(o, lhsT=identb[:, :], rhs=mr4b[:, :], start=False, stop=True),
                         reads=[b_c2], writes=[bb])
                    S.op("act", lambda e, o=LT2[:, g * 512:(g + 1) * 512], i=bk[:, :]:
                         e.activation(o, i, AF.Exp), reads=[bb], writes=[b_LT])
                    yield
                bkc, bbc = nb()
                for g in range(4):
                    S.op("pe", lambda e, o=bkc[:, g * 128:(g + 1) * 128], l=xBCT[:, 8 + g, tc], r=xBCT[:, 12 + g, tc]:
                         e.matmul(o, lhsT=l, rhs=r, start=True, stop=True), reads=[b_xbc[8 + g], b_xbc[12 + g]], writes=[bbc])
                for g in range(4):
                    S.op("dve", lambda e, o=MT[:, 4 * g:4 * g + 4, :], i=LT[:, 4 * g:4 * g + 4, :],
                         c=bkc[:, g * 128:(g + 1) * 128].unsqueeze(1).to_broadcast([128, 4, 128]):
                         e.tensor_tensor(o, i, c, ALU.mult), reads=[b_LT, bbc], writes=[b_MT])
                yield
                yd = [nb(hold=True), nb(hold=True)]
                for h in range(16):
                    bk, bb = yd[h // 8]
                    S.op("pe", lambda e, o=bk[:, (h % 8) * 64:(h % 8) * 64 + 64], l=MT[:, h, :], r=Xd[:, h * 64:(h + 1) * 64]:
                         e.matmul(o, lhsT=l, rhs=r, start=True, stop=True), reads=[b_MT, b_Xd], writes=[bb])
                yield
                yo = [nb(hold=True), nb(hold=True)]
                for g in range(4):
                    bk, bb = yo[g // 2]
                    S.op("pe", lambda e, o=bk[:, (g % 2) * 256:(g % 2) * 256 + 256], l=xBCT[:, 12 + g, tc], r=Sbf[:, g * 256:(g + 1) * 256]:
                         e.matmul(o, lhsT=l, rhs=r, start=True, stop=True), reads=[b_xbc[12 + g], b_Sbf], writes=[bb])
                yield
                for hf in range(2):
                    S.op("dve", lambda e, o=Yt[:, hf * 512:(hf + 1) * 512].rearrange("p (h d) -> p h d", h=8),
                         i=yo[hf][0][:, :].rearrange("p (h d) -> p h d", h=8),
                         s=ex[:, hf * 8:hf * 8 + 8].unsqueeze(2).to_broadcast([128, 8, 64]):
                         e.tensor_tensor(o, i, s, ALU.mult), reads=[yo[hf][1], b_ex], writes=[b_Y])
                    S.op("dve", lambda e, o=Yt[:, hf * 512:(hf + 1) * 512], i=yd[hf][0][:, :]:
                         e.tensor_tensor(o, o, i, ALU.add), reads=[yd[hf][1], b_Y], writes=[b_Y])
                rel(*yd)
                rel(*yo)
                yield
                S.op("dve", lambda e: e.tensor_tensor(XDk.rearrange("p (h d) -> p h d", h=16), xt3,
                                                      prm[:, P_DSK:P_DSK + 16].unsqueeze(2).to_broadcast([128, 16, 64]), ALU.mult),
                     reads=[b_xtok[t], b_prm], writes=[b_XDk])
                S.op("dve", lambda e: e.tensor_tensor(Yt, Yt, XDk, ALU.add), reads=[b_Y, b_XDk], writes=[b_Y])
                S.op("dve", lambda e: e.tensor_tensor(Yt, Yt, sz[:, t, :], ALU.mult), reads=[b_Y, b_sz[t]], writes=[b_Y])
                yield
                for g in range(4):
                    S.op("act", lambda e, i=Yt[:, g * 256:(g + 1) * 256], o=junk[:, 0:256], a=gss[:, g:g + 1]:
                         e.activation(o, i, AF.Square, accum_out=a), reads=[b_Y], writes=[b_junk, b_gss])
                S.op("act", lambda e: e.activation(gss[:, 4:8], gss[:, 0:4], AF.Ln, bias=eps_ap, scale=1.0 / 256),
                     reads=[b_gss, b_c2], writes=[b_gss])
                S.op("act", lambda e: e.activation(gss[:, 8:12], gss[:, 4:8], AF.Exp, scale=-0.5), reads=[b_gss], writes=[b_gss])
                yti = ytile[t % 2]
                S.op("dve", lambda e, o=yti[:, 0:1024].rearrange("p (g c) -> p g c", g=4):
                     e.tensor_tensor(o, Yt.rearrange("p (g c) -> p g c", g=4), gss[:, 8:12].unsqueeze(2).to_broadcast([128, 4, 256]), ALU.mult),
                     reads=[b_Y, b_gss], writes=[b_yt[t % 2][0]])

            yield
            chk(6)
            st = [nb(), nb()]
            for g in range(4):
                bk, bb = st[g // 2]
                S.op("pe", lambda e, o=bk[:, (g % 2) * 256:(g % 2) * 256 + 256], l=Btok[:, t, g * 128:(g + 1) * 128], r=Xdd[:, g * 256:(g + 1) * 256]:
                     e.matmul(o, lhsT=l, rhs=r, start=True, stop=True), reads=[b_btok[t], b_Xdd], writes=[bb])
            S.op("dve", lambda e: e.tensor_tensor(Sst[:, :].rearrange("p (h d) -> p h d", h=16),
                                                  Sst[:, :].rearrange("p (h d) -> p h d", h=16), cdec_b, ALU.mult),
                 reads=[b_S, b_ex], writes=[b_S])
            for hf in range(2):
                S.op("dve", lambda e, o=Sst[:, hf * 512:(hf + 1) * 512], i=st[hf][0][:, :]: e.tensor_tensor(o, o, i, ALU.add),
                     reads=[b_S, st[hf][1]], writes=[b_S])
            if own:
                S.op("dve", lambda e: e.tensor_copy(Sbf[:, :], Sst[:, :]), reads=[b_S], writes=[b_Sbf])

            yield

        def attn_gen(t):
            tc = slice(t * 128, (t + 1) * 128)
            yti = ytile[t % 2]
            chk(7)
            if own:
                mk = am0b if (first_own and t == 0) else amb
                for j in range(8):
                    bk, bb = nb()
                    for half in range(2):
                        h = 2 * j + half
                        pad = KP[(h // 8) * 2 + half]
                        S.op("pe", lambda e, o=bk[:, half * 256:(half + 1) * 256], l=qT[:, j, tc], r=pad[:, t * 128:(t + 2) * 128]:
                             e.matmul(o, lhsT=l, rhs=r, start=True, stop=False), reads=[b_qT[j], b_KP[t], b_KP[t + 1]], writes=[bb])
                        S.op("pe", lambda e, o=bk[:, half * 256:(half + 1) * 256]: e.matmul(o, lhsT=identb[:, :], rhs=mk[:, :], start=False, stop=True),
                             reads=[b_c2], writes=[bb])
                    S.op("dve", lambda e, o=rmax[:, 2 * j:2 * j + 2], i=bk[:, :].rearrange("p (a k) -> p a k", a=2):
                         e.tensor_reduce(o, i, AX.X, ALU.max), reads=[bb], writes=[b_sm])
                    S.op("dve", lambda e, o=mrow[:, 2 * j:2 * j + 2], i=rmax[:, 2 * j:2 * j + 2], s=prm[:, P_SNK + 2 * j:P_SNK + 2 * j + 2]:
                         e.scalar_tensor_tensor(o, i, 0.125, s, ALU.mult, ALU.max), reads=[b_sm, b_prm], writes=[b_sm])
                    S.op("dve", lambda e, o=negm[:, 2 * j:2 * j + 2], i=mrow[:, 2 * j:2 * j + 2]: e.tensor_scalar(o, i, -1.0, None, ALU.mult),
                         reads=[b_sm], writes=[b_sm])
                    for half in range(2):
                        h = 2 * j + half
                        S.op("act", lambda e, o=pS[:, h, :], i=bk[:, half * 256:(half + 1) * 256], b=negm[:, h:h + 1], a=rsum[:, h:h + 1]:
                             e.activation(o, i, AF.Exp, bias=b, scale=0.125, accum_out=a), reads=[bb, b_sm], writes=[b_p, b_sm])
                    yield
                S.op("dve", lambda e: e.tensor_tensor(esk[:, :], prm[:, P_SNK:P_SNK + 16], mrow[:, :], ALU.subtract), reads=[b_sm, b_prm], writes=[b_sm])
                S.op("act", lambda e: e.activation(esk[:, :], esk[:, :], AF.Exp), reads=[b_sm], writes=[b_sm])
                S.op("dve", lambda e: e.tensor_tensor(rden[:, :], rsum[:, :], esk[:, :], ALU.add), reads=[b_sm], writes=[b_sm])
                S.op("dve", lambda e: e.reciprocal(rden[:, :], rden[:, :]), reads=[b_sm], writes=[b_sm])
                for q4 in range(4):
                    bk, bb = nb()
                    bkb = bk[:, :].bitcast(BF16)
                    for hh in range(4):
                        h = q4 * 4 + hh
                        for kb in range(2):
                            S.op("pe", lambda e, o=bkb[:, hh * 256 + kb * 128:hh * 256 + kb * 128 + 128], i=pS[:, h, kb * 128:(kb + 1) * 128]:
                                 e.transpose(o, i, identb[:, :]), reads=[b_p, b_c2], writes=[bb])
                    S.op("act", lambda e, o=pTt[:, q4 * 4:q4 * 4 + 4, :], i=bkb.rearrange("p (h k) -> p h k", h=4): e.copy(o, i),
                         reads=[bb], writes=[b_pT])
                    yield
                po = [nb(hold=True), nb(hold=True)]
                for h in range(16):
                    bk, bb = po[h // 8]
                    kv = h // 8
                    o = bk[:, (h % 8) * 64:(h % 8) * 64 + 64]
                    S.op("pe", lambda e, o=o, l=pTt[:, h, 0:128], r=vtok[:, t, kv * 64:(kv + 1) * 64]:
                         e.matmul(o, lhsT=l, rhs=r, start=True, stop=False), reads=[b_pT, b_v[t]], writes=[bb])
                    S.op("pe", lambda e, o=o, l=pTt[:, h, 128:256], r=vtok[:, t + 1, kv * 64:(kv + 1) * 64]:
                         e.matmul(o, lhsT=l, rhs=r, start=False, stop=True), reads=[b_pT, b_v[t + 1]], writes=[bb])
                yield
                for hf in range(2):
                    S.op("dve", lambda e, o=OA[:, hf * 512:(hf + 1) * 512].rearrange("p (h d) -> p h d", h=8),
                         i=po[hf][0][:, :].rearrange("p (h d) -> p h d", h=8),
                         s=rden[:, hf * 8:hf * 8 + 8].unsqueeze(2).to_broadcast([128, 8, 64]):
                         e.tensor_tensor(o, i, s, ALU.mult), reads=[po[hf][1], b_sm], writes=[b_OA])
                rel(*po)
                S.op("act", lambda e: e.activation(junk[:, 0:1024], OA, AF.Square, accum_out=gss[:, 12:13]), reads=[b_OA], writes=[b_junk, b_gss])
                S.op("act", lambda e: e.activation(gss[:, 13:14], gss[:, 12:13], AF.Ln, bias=eps_ap, scale=1.0 / 1024),
                     reads=[b_gss, b_c2], writes=[b_gss])
                S.op("act", lambda e: e.activation(gss[:, 14:15], gss[:, 13:14], AF.Exp, scale=-0.5), reads=[b_gss], writes=[b_gss])
                S.op("act", lambda e, o=yti[:, 1024:2048]: e.activation(o, OA, AF.Copy, scale=gss[:, 14:15]),
                     reads=[b_OA, b_gss], writes=[b_yt[t % 2][1]])
            yield

        def ytrans(t):
            tc = slice(t * 128, (t + 1) * 128)
            yti = ytile[t % 2]
            yield
            if own:
                chk(8)
                for half in range(2):
                    bk, bb = nb()
                    bkb = bk[:, :].bitcast(BF16)
                    for kc in range(8):
                        c = half * 8 + kc
                        S.op("pe", lambda e, o=bkb[:, kc * 128:(kc + 1) * 128], i=yti[:, c * 128:(c + 1) * 128]: e.transpose(o, i, identb[:, :]),
                             reads=[b_yt[t % 2][half], b_c2], writes=[bb])
                    S.op("dve", lambda e, o=hy[:, half * 8:(half + 1) * 8, tc], i=bkb.rearrange("p (a b) -> p a b", a=8),
                         g=prm[:, P_GC + half * 8:P_GC + (half + 1) * 8].unsqueeze(2).to_broadcast([128, 8, 128]):
                         e.tensor_tensor(o, i, g, ALU.mult), reads=[bb, b_prm], writes=[b_hy[t]])
                    yield

        def qproj_gen():
            for (col0, cbase) in ((3088, 0), (3600, 4)):
                W, bW = load_w([(0, 512, wblk(w_in, WIN_BLK[col0]))])
                for cl in range(4):
                    c = cbase + cl
                    bk, bb = nb(hold=True)
                    for kc in range(16):
                        S.op("pe", lambda e, o=bk[:, :], l=W[:, kc, cl * 128:(cl + 1) * 128], r=hy[:, kc, :], kc=kc:
                             e.matmul(o, lhsT=l, rhs=r, start=(kc == 0), stop=(kc == 15)), reads=[bW[kc // 8]] + b_hy, writes=[bb])
                        if kc % 4 == 3:
                            yield
                    S.op("act", lambda e, o=qT[:, c, :], i=bk[:, :]: e.copy(o, i), reads=[bb], writes=[b_qT[c]])
                    rel((bk, bb))
                    yield

        def ssd_all():
            for t in range(4):
                yield from ssd_gen(t)

        if own:
            run_il([ssd_gen(0), qproj_gen()])
            pend = None
            for t in range(4):
                run_il([attn_gen(t)] + ([pend] if pend is not None else []) + ([ssd_gen(t + 1)] if t < 3 else []))
                pend = ytrans(t)
            run_il([pend])
        else:
            run_il([ssd_all()] + ([next_m1] if next_m1 is not None else []))
        if own or last_prefix:
            for i in range(4):
                S.op("dve", lambda e, i=i: e.tensor_copy(KP[i][:, 0:128], KP[i][:, 512:640]), reads=[b_KP[4]], writes=[b_KP[0]])
            S.op("dve", lambda e: e.tensor_copy(vtok[:, 0, :], vtok[:, 4, :]), reads=[b_v[4]], writes=[b_v[0]])

    def outproj_mlp(xsrc, osrc, next_xsrc=None):
        chk(9)
        mix_alias = b_xbc + b_xtok + b_btok + b_cv
        fence(mix_alias, b_x1)
        for t in range(4):
            S.dma("sp", "x1_%d" % t, lambda e, o=x1v[:, t, :], s=xsrc[t * 128:(t + 1) * 128, :]: e.dma_start(out=o, in_=s), writes=[b_x1[t]])
        for db in range(4):
            W, bW = load_w([(0, 512, wblk(w_out, db))])
            for t in range(4):
                bk, bb = nb()
                for kc in range(16):
                    S.op("pe", lambda e, o=bk[:, :], l=hy[:, kc, t * 128:(t + 1) * 128], r=W[:, kc, 0:512], kc=kc:
                         e.matmul(o, lhsT=l, rhs=r, start=(kc == 0), stop=(kc == 15)), reads=[bW[kc // 8], b_hy[t]], writes=[bb])
                S.op("dve", lambda e, o=x1v[:, t, db * 512:(db + 1) * 512], i=bk[:, :]: e.tensor_tensor(o, o, i, ALU.add),
                     reads=[bb, b_x1[t]], writes=[b_x1[t]])
        chk(10)
        fence(mixer_bufs, mlp_bufs)
        for t in range(4):
            rms_T(x1v[:, t, :], b_x1[t], P_G2, h2T, b_h2, t, xn2, b_xn2, ss2, b_ss2)
        chk(11)
        for fb in range(16):
            W, bW = load_w([(0, 512, wblk(w_up, fb))])
            for sub in range(4):
                fc = fb * 4 + sub
                bk, bb = nb()
                for kc in range(16):
                    S.op("pe", lambda e, o=bk[:, :], l=W[:, kc, sub * 128:(sub + 1) * 128], r=h2T[:, kc, :], kc=kc:
                         e.matmul(o, lhsT=l, rhs=r, start=(kc == 0), stop=(kc == 15)), reads=[bW[kc // 8]] + b_h2, writes=[bb])
                r_, br_ = rl[fc % 2], b_rl[fc % 2]
                S.op("act", lambda e, o=r_, i=bk[:, :]: e.activation(o, i, AF.Relu), reads=[bb], writes=[br_])
                S.op("dve", lambda e, o=aT[:, fc, :], i=r_: e.tensor_tensor(o, i, i, ALU.mult), reads=[br_], writes=[b_aT[fc]])
        chk(12)

        def down_gen():
            for db in range(4):
                acc = [nb(hold=True) for _ in range(4)]
                for fblk in range(4):
                    W, bW = load_w([(0, 512, wblk(w_down, db * 4 + fblk))])
                    for t in range(4):
                        bk, bb = acc[t]
                        for kc in range(16):
                            fc = fblk * 16 + kc
                            S.op("pe", lambda e, o=bk[:, :], l=aT[:, fc, t * 128:(t + 1) * 128], r=W[:, kc, 0:512], fc=fc:
                                 e.matmul(o, lhsT=l, rhs=r, start=(fc == 0), stop=(fc == 63)), reads=[bW[kc // 8], b_aT[fc]], writes=[bb])
                            if kc % 8 == 7:
                                yield
                for t in range(4):
                    bk, bb = acc[t]
                    S.op("dve", lambda e, o=x1v[:, t, db * 512:(db + 1) * 512], i=bk[:, :]: e.tensor_tensor(o, o, i, ALU.add),
                         reads=[bb, b_x1[t]], writes=[b_x1[t]])
                rel(*acc)
                yield

        gens = [down_gen()]
        if next_xsrc is not None:
            fence(b_h2, [b_xth])
            gens.append(m1_gen(next_xsrc, [xth], [b_xth], xn2, b_xn2, ss2, b_ss2, "xth"))
        run_il(gens)
        chk(13)
        outs = []
        for t in range(4):
            S.op("act", lambda e, i=x1v[:, t, :]: e.activation(xn2[:, :], i, AF.Square, accum_out=ss2[:, 0:1]),
                 reads=[b_x1[t]], writes=[b_xn2, b_ss2])
            S.op("act", lambda e: e.activation(ss2[:, 1:2], ss2[:, 0:1], AF.Ln, bias=eps_ap, scale=1.0 / 2048), reads=[b_ss2, b_c2], writes=[b_ss2])
            S.op("act", lambda e: e.activation(ss2[:, 2:3], ss2[:, 1:2], AF.Exp, scale=-0.5), reads=[b_ss2], writes=[b_ss2])
            S.op("dve", lambda e, i=x1v[:, t, :]: e.scalar_tensor_tensor(ot, i, ss2[:, 2:3], fgr[:, :], ALU.mult, ALU.mult),
                 reads=[b_x1[t], b_ss2, b_fgr], writes=b_h2)
            d = S.dma("sp", "out", lambda e, o=osrc[t * 128:(t + 1) * 128, :]: e.dma_start(out=o, in_=ot), reads=b_h2)
            outs.append(d)
        fence(mlp_bufs, mixer_bufs)
        fence(b_x1, mix_alias)
        return outs

    all_out = []
    try:
        srcs = [(xp[pb * 512:(pb + 1) * 512, :], "prefix_last" if pb == n_prefix_blocks - 1 else "prefix") for pb in range(n_prefix_blocks)]
        srcs += [(xo[ob * 512:(ob + 1) * 512, :], "own") for ob in range(n_own_blocks)]
        hoisted = False
        nown = 0
        for bi, (src, kind) in enumerate(srcs):
            nxt = srcs[bi + 1][0] if bi + 1 < len(srcs) else None
            if kind != "own":
                mixer_block(src, kind, False, skip_m1=hoisted, next_m1=(std_m1(nxt) if nxt is not None else None))
                hoisted = nxt is not None
                if kind == "prefix_last":
                    S.op("dve", lambda e: e.tensor_scalar(Sst[:, :], Sst[:, :], flg[:, 0:1], None, ALU.mult), reads=[b_S, b_flg], writes=[b_S])
                    S.op("dve", lambda e: e.tensor_copy(Sbf[:, :], Sst[:, :]), reads=[b_S], writes=[b_Sbf])
            else:
                mixer_block(src, kind, nown == 0, skip_m1=hoisted)
                all_out += outproj_mlp(src, out_d[nown * 512:(nown + 1) * 512, :], next_xsrc=nxt)
                hoisted = nxt is not None
                nown += 1
    except _Stop:
        pass
    if dumps:
        allb = [b_prm, b_cst, b_c2, b_S, b_Sbf] + b_tail + b_KP + b_v + b_W + b_hy + b_x1 + b_xbc + b_xtok + b_btok + b_cv + mixer_bufs + mlp_bufs
        loc = dict(hy=hy, X1=X1, AR=AR, Sst=Sst, Sbf=Sbf, KP0=KP[0], KP1=KP[1], KP2=KP[2], KP3=KP[3], vtok=vtok, xBCT=xBCT, xtok=xtok, Btok=Btok,
                   sz=sz, qT=qT, dtt=dtt, att=att, ex=ex, LT2=LT2, MT=MT, Yt=Yt, yt0=ytile[0], yt1=ytile[1], OA=OA, pS=pS, h2T=h2T, aT=aT, tailu=tailu,
                   rden=rden, mrow=mrow, rsum=rsum, gss=gss, Xd=Xd, Xdd=Xdd, pTt=pTt)
        for name in dumps:
            t_ = loc[name]
            src = t_ if isinstance(t_, bass.AP) else t_.ap()
            shp = list(src.shape)
            d_ = nc.dram_tensor("dbg_" + name, shp, src.dtype, kind="ExternalOutput").ap()
            all_out.append(S.dma("sp", "dbg_" + name, lambda e, o=d_, s_=src: e.dma_start(out=o, in_=s_), reads=allb))
    S.barrier_wait("sp", all_out)
    emit(nc, S)
    return nc


def _pack_params(inp):
    f = np.float32
    prm = np.zeros((128, NPRM), f)
    prm[:, P_G1:P_G1 + 16] = inp["mix_norm_g"][0].reshape(16, 128).T
    prm[:, P_G2:P_G2 + 16] = inp["mlp_norm_g"][0].reshape(16, 128).T
    prm[:, P_GC:P_GC + 8] = inp["ssm_norm_g"][0].reshape(8, 128).T
    prm[:, P_GC + 8:P_GC + 16] = inp["attn_out_norm_g"][0].reshape(8, 128).T
    cw = inp["conv_w"][0]
    prm[:, P_CW:P_CW + 64] = cw.reshape(4, 16, 128).transpose(2, 1, 0).reshape(128, 64)
    prm[:, P_CB:P_CB + 16] = inp["conv_b"][0].reshape(16, 128).T
    prm[:, P_DTB:P_DTB + 16] = np.broadcast_to(inp["dt_bias"][0], (128, 16))
    prm[:, P_ALOG:P_ALOG + 16] = np.broadcast_to(inp["A_log"][0], (128, 16))
    prm[:, P_DSK:P_DSK + 16] = np.broadcast_to(inp["D_skip"][0], (128, 16))
    prm[:, P_SNK:P_SNK + 16] = np.broadcast_to(inp["attn_sinks"][0], (128, 16))
    return prm


def _consts(first_half):
    f = np.float32
    cst = np.zeros((128, NCST), f)
    i = np.arange(128)
    cst[:, C_ID:C_ID + 128] = np.eye(128, dtype=f)
    cst[:, C_TRI:C_TRI + 128] = (i[:, None] <= i[None, :]).astype(f)
    cst[:, C_US:C_US + 128] = (i[:, None] > i[None, :]).astype(f)
    m = np.where(i[None, :] >= i[:, None], 0.0, NEG).astype(f)
    cst[:, C_MR4:C_MR4 + 512] = np.tile(m, (1, 4))
    j = np.arange(256)
    diff = 128 + i[:, None] - j[None, :]
    am = np.where((diff >= 0) & (diff < 128), 0.0, NEG).astype(f)
    cst[:, C_AM:C_AM + 256] = am
    am0 = am.copy()
    if first_half:
        am0[:, 0:128] = NEG
    cst[:, C_AM0:C_AM0 + 256] = am0
    return cst


_NC_CACHE = {}


def _blk(W, r0, c0, ncols=512):
    return np.ascontiguousarray(W[r0:r0 + 2048, c0:c0 + ncols].reshape(16, 128, ncols).transpose(1, 0, 2)).reshape(128, 16 * ncols)


def make_in_maps(inputs, cores=range(8)):
    inp = {k: np.asarray(v) for k, v in inputs.items()}
    x = np.ascontiguousarray(inp["x"], dtype=np.float32)
    prm = _pack_params(inp)
    fgr = np.ascontiguousarray(np.broadcast_to(inp["final_norm_g"].astype(np.float32), (128, 2048)))
    w_in = np.asarray(inp["w_in"][0], dtype=np.float32)
    w_out = np.asarray(inp["w_out"][0], dtype=np.float32)
    w_up = np.asarray(inp["w_up"][0], dtype=np.float32)
    w_down = np.asarray(inp["w_down"][0], dtype=np.float32)
    w_in_p = np.stack([_blk(w_in, 0, c0) for c0 in (1024, 1536, 2048, 2560, 3088, 3600, 0, 512)])
    w_kv = _blk(w_in, 0, 4112, 256)
    w_out_p = np.stack([_blk(w_out, 0, db * 512) for db in range(4)])
    w_up_p = np.stack([_blk(w_up, 0, fb * 512) for fb in range(16)])
    w_down_p = np.stack([_blk(w_down, fblk * 2048, db * 512) for db in range(4) for fblk in range(4)])
    wdt = np.ascontiguousarray(w_in[:, 3072:3088].reshape(16, 128, 16).transpose(1, 0, 2).reshape(128, 256))
    in_maps = []
    for c in cores:
        b, half = c // 2, c % 2
        xo = x[b, half * 1024:(half + 1) * 1024]
        xp = x[b, 0:1024] if half == 1 else np.zeros((1024, 2048), np.float32)
        in_maps.append({
            "xo": np.ascontiguousarray(xo), "xp": np.ascontiguousarray(xp),
            "w_in": w_in_p, "w_kv": w_kv, "w_out": w_out_p, "w_up": w_up_p, "w_down": w_down_p,
            "prm": prm, "cst": _consts(half == 0), "fgr": fgr,
            "flg": np.full((128, 1), float(half), np.float32),
            "wdt": wdt,
        })
    return in_maps


def kernel(**inputs):
    if "nc" not in _NC_CACHE:
        _NC_CACHE["nc"] = build_nc()
    nc = _NC_CACHE["nc"]
    in_maps = make_in_maps(inputs)
    res = run_bass_kernel_spmd(nc, in_maps, core_ids=list(range(8)))
    out = np.zeros((4, 2048, 2048), np.float32)
    for c in range(8):
        b, half = c // 2, c % 2
        out[b, half * 1024:(half + 1) * 1024] = res.results[c]["out"]
    return out
```
